# Optimizing a Trainium2 kernel written in Bass

```python
import jax, jax.numpy as jnp
from jax import lax
import numpy as np

D_MODEL = 1024
BATCH = 32
SEQ = 2048
DEPTH = 1

D_MIX = D_MODEL
D_REC = D_MIX // 2
D_POOL = D_MIX - D_REC
REC_HEAD_DIM = 128
N_REC_HEADS = D_REC // REC_HEAD_DIM
POOL_WINDOWS = (2, 4, 8, 16)
N_POOL_GROUPS = len(POOL_WINDOWS)
POOL_GROUP_DIM = D_POOL // N_POOL_GROUPS
D_IN = 4 * D_REC + D_POOL
D_FF = 4 * D_MODEL
N_MOD = 6
CHUNK = 32
EPS = 1e-6

kernel_name = "hybrid_hgrn2_pool_adaln_block"


def _rmsnorm(x, w):
    xf = x.astype(jnp.float32)
    xf = xf * lax.rsqrt(jnp.mean(xf * xf, axis=-1, keepdims=True) + EPS)
    return xf * w.astype(jnp.float32)


def _hgrn2(q, f_logit, v, g, lb, g_norm_w):
    B, T, _ = q.shape
    H, Dh, C = N_REC_HEADS, REC_HEAD_DIM, CHUNK
    n = T // C
    f32 = jnp.float32
    forget = lb + (1.0 - lb) * jax.nn.sigmoid(f_logit.astype(f32))
    k = 1.0 - forget
    logf = jnp.log(forget)
    qs = jax.nn.silu(q.astype(f32)) * (Dh ** -0.5)

    def split(a):
        return a.reshape(B, n, C, H, Dh).transpose(0, 3, 1, 2, 4)

    qs, k, vv, logf = (split(a) for a in (qs, k, v.astype(f32), logf))
    b = jnp.cumsum(logf, axis=3)
    b_ref = b[:, :, :, C // 2 - 1:C // 2]
    b_last = b[:, :, :, C - 1:C]

    scores = jnp.einsum('bhncd,bhnsd->bhncs', qs * jnp.exp(b - b_ref), k * jnp.exp(b_ref - b))
    causal = jnp.tril(jnp.ones((C, C), dtype=bool))
    scores = jnp.where(causal, scores, 0.0)
    o_intra = jnp.einsum('bhncs,bhnse->bhnce', scores, vv)

    q_in = qs * jnp.exp(b)
    k_out = k * jnp.exp(b_last - b)
    decay_chunk = jnp.exp(b_last[:, :, :, 0])

    def step(S, xs):
        q_c, k_c, v_c, d_c = xs
        o = jnp.einsum('bhcd,bhde->bhce', q_c, S)
        S = d_c[..., None] * S + jnp.einsum('bhcd,bhce->bhde', k_c, v_c)
        return S, o

    S0 = jnp.zeros((B, H, Dh, Dh), f32)
    xs = tuple(jnp.moveaxis(a, 2, 0) for a in (q_in, k_out, vv, decay_chunk))
    _, o_inter = lax.scan(step, S0, xs)
    o = o_intra + jnp.moveaxis(o_inter, 0, 2)
    o = o.transpose(0, 2, 3, 1, 4).reshape(B, T, H, Dh)

    gg = g.astype(f32).reshape(B, T, H, Dh)
    o = o * lax.rsqrt(jnp.mean(o * o, axis=-1, keepdims=True) + EPS) * g_norm_w.astype(f32) * jax.nn.silu(gg)
    return o.reshape(B, T, D_REC)


def _pool_mixer(p, w_pool, pool_scale):
    B, T, _ = p.shape
    G, Dg = N_POOL_GROUPS, POOL_GROUP_DIM
    W_MAX = max(POOL_WINDOWS)
    pf = p.astype(jnp.float32)
    cs = jnp.cumsum(pf, axis=1)
    cs_pad = jnp.pad(cs, ((0, 0), (W_MAX, 0), (0, 0)))
    pos = jnp.arange(T)
    outs = []
    for gi, w in enumerate(POOL_WINDOWS):
        sl = slice(gi * Dg, (gi + 1) * Dg)
        lower = cs_pad[:, W_MAX - w:W_MAX - w + T, sl]
        count = jnp.minimum(pos + 1, w).astype(jnp.float32)[None, :, None]
        outs.append((cs[:, :, sl] - lower) / count - pf[:, :, sl])
    pooled = jnp.stack(outs, axis=2)
    mixed = jnp.einsum('btgc,gcd->btgd', pooled, w_pool.astype(jnp.float32))
    return mixed.reshape(B, T, D_POOL) * pool_scale.astype(jnp.float32)


def setup_inputs(seed: int = 0) -> dict:
    key = jax.random.key(seed)
    ks = jax.random.split(key, 16)
    f32 = jnp.float32
    nrm = lambda k, shape, s: jax.random.normal(k, shape, f32) * s
    return {
        "x": nrm(ks[0], (BATCH, SEQ, D_MODEL), 1.0),
        "c": nrm(ks[1], (BATCH, D_MODEL), 1.0),
        "w_ada": nrm(ks[2], (DEPTH, D_MODEL, N_MOD * D_MODEL), 0.5 * D_MODEL ** -0.5),
        "b_ada": nrm(ks[3], (DEPTH, N_MOD * D_MODEL), 0.01),
        "norm_mix_w": 1.0 + nrm(ks[4], (DEPTH, D_MODEL), 0.02),
        "w_in": nrm(ks[5], (DEPTH, D_MODEL, D_IN), D_MODEL ** -0.5),
        "lb_logits": nrm(ks[6], (DEPTH + 1, D_REC), 0.1),
        "g_norm_w": 1.0 + nrm(ks[7], (DEPTH, REC_HEAD_DIM), 0.02),
        "w_pool": nrm(ks[8], (DEPTH, N_POOL_GROUPS, POOL_GROUP_DIM, POOL_GROUP_DIM), POOL_GROUP_DIM ** -0.5),
        "pool_scale": 1.0 + nrm(ks[9], (DEPTH, D_POOL), 0.02),
        "w_out": nrm(ks[10], (DEPTH, D_MIX, D_MODEL), D_MIX ** -0.5),
        "norm_mlp_w": 1.0 + nrm(ks[11], (DEPTH, D_MODEL), 0.02),
        "w_up": nrm(ks[12], (DEPTH, D_MODEL, D_FF), D_MODEL ** -0.5),
        "w_down": nrm(ks[13], (DEPTH, D_FF, D_MODEL), D_FF ** -0.5),
        "norm_final_w": 1.0 + nrm(ks[14], (D_MODEL,), 0.02),
    }


def reference(x, c, w_ada, b_ada, norm_mix_w, w_in, lb_logits, g_norm_w, w_pool, pool_scale,
              w_out, norm_mlp_w, w_up, w_down, norm_final_w):
    dtype = x.dtype
    f32 = jnp.float32
    lbs = jnp.cumsum(jax.nn.softmax(lb_logits.astype(f32), axis=0), axis=0)
    c_act = jax.nn.silu(c.astype(f32))
    h = x.astype(f32)
    for l in range(DEPTH):
        mod = c_act @ w_ada[l].astype(f32) + b_ada[l].astype(f32)
        sh_a, sc_a, gt_a, sh_m, sc_m, gt_m = (m[:, None, :] for m in jnp.split(mod, N_MOD, axis=-1))

        u = _rmsnorm(h, norm_mix_w[l]) * (1.0 + sc_a) + sh_a
        proj = u @ w_in[l].astype(f32)
        q, f_logit, v, g, p = jnp.split(proj, [D_REC, 2 * D_REC, 3 * D_REC, 4 * D_REC], axis=-1)
        o_rec = _hgrn2(q, f_logit, v, g, lbs[l], g_norm_w[l])
        o_pool = _pool_mixer(p, w_pool[l], pool_scale[l])
        mix = jnp.concatenate([o_rec, o_pool], axis=-1) @ w_out[l].astype(f32)
        h = h + gt_a * mix

        u = _rmsnorm(h, norm_mlp_w[l]) * (1.0 + sc_m) + sh_m
        hid = jnp.square(jax.nn.relu(u @ w_up[l].astype(f32)))
        h = h + gt_m * (hid @ w_down[l].astype(f32))
    return _rmsnorm(h, norm_final_w).astype(dtype)
```

```python
import numpy as np
from contextlib import ExitStack
import concourse.bass as bass
import concourse.mybir as mybir
from concourse.bass_utils import run_bass_kernel_spmd

F32 = mybir.dt.float32
BF16 = mybir.dt.bfloat16
AF = mybir.ActivationFunctionType
ALU = mybir.AluOpType

ENGS = ("pe", "act", "dve", "pool", "sp")
EPS = 1e-6
NCORES = 8
SEQ = 2048
D = 1024
TOK = 4 * SEQ
TT = 512
NT_FULL = TOK // TT
NW = 6
NXS = 8
CQ = float(128 ** -0.5)
LEAD = 9
TAIL = 6


class Buf:
    __slots__ = ("name", "w", "r", "excl", "multi")

    def __init__(self, name, excl=False, multi=False):
        self.name = name
        self.w = None
        self.r = []
        self.excl = excl
        self.multi = multi


class Op:
    __slots__ = ("eng", "fn", "waits", "signal", "cnt", "dma_sem", "dma_val", "label")

    def __init__(self, eng, fn):
        self.eng = eng
        self.fn = fn
        self.waits = []
        self.signal = False
        self.cnt = 0
        self.dma_sem = None
        self.dma_val = 0


class Prog:
    def __init__(self):
        self.ops = {e: [] for e in ENGS}
        self.dma_counts = {}
        self.label = ""


    def _add(self, eng, fn, reads, writes, dma_key=None):
        op = Op(eng, fn)
        op.label = self.label
        lst = self.ops[eng]
        idx = len(lst)
        deps = []
        for b in reads:
            if b.excl:
                continue
            if b.w is not None:
                deps.append(b.w)
        wr = list(writes) + [b for b in reads if b.excl]
        for b in wr:
            if b.w is not None and not (b.multi and dma_key is not None):
                deps.append(b.w)
            deps.extend(b.r)
        if dma_key is not None:
            n = self.dma_counts.get(dma_key, 0) + 1
            self.dma_counts[dma_key] = n
            op.dma_sem = dma_key
            op.dma_val = 16 * n
            ev = ("dma", dma_key, 16 * n)
        else:
            ev = ("eng", eng, idx)
        best = {}
        for d in deps:
            k = (d[0], d[1])
            if k not in best or d[2] > best[k][2]:
                best[k] = d
        for k, d in best.items():
            if d[0] == "eng" and d[1] == eng and eng == "pe":
                continue
            op.waits.append(d)
        for b in reads:
            if not b.excl:
                b.r = [r for r in b.r if (r[0], r[1]) != (ev[0], ev[1])] + [ev]
        for b in wr:
            b.w = ev
            b.r = []
        lst.append(op)
        return op

    def pe(self, fn, reads=(), writes=()):
        return self._add("pe", fn, reads, writes)

    def act(self, fn, reads=(), writes=()):
        return self._add("act", fn, reads, writes)

    def dve(self, fn, reads=(), writes=()):
        return self._add("dve", fn, reads, writes)

    def pool(self, fn, reads=(), writes=()):
        return self._add("pool", fn, reads, writes)

    def dma(self, eng, fn, reads=(), writes=(), key=None):
        return self._add(eng, fn, reads, writes, dma_key=key)

    def emit(self, nc, final_dma_keys=()):
        for e in ENGS:
            for op in self.ops[e]:
                for d in op.waits:
                    if d[0] == "eng":
                        self.ops[d[1]][d[2]].signal = True
        for e in ENGS:
            c = 0
            for op in self.ops[e]:
                if op.signal:
                    c += 1
                op.cnt = c
        dma_keys = sorted(self.dma_counts.keys())
        with ExitStack() as st:
            esem = {e: st.enter_context(nc.semaphore("s_" + e)) for e in ENGS}
            dsem = {k: st.enter_context(nc.semaphore("d_" + str(k))) for k in dma_keys}
            block = st.enter_context(nc.Block())
            prog = self

            def run(e, eng):
                waited = {}
                for op in prog.ops[e]:
                    for d in op.waits:
                        if d[0] == "eng":
                            tgt = prog.ops[d[1]][d[2]]
                            key, val, sem = ("e", d[1]), tgt.cnt, esem[d[1]]
                        else:
                            key, val, sem = ("d", d[1]), d[2], dsem[d[1]]
                        if waited.get(key, 0) >= val:
                            continue
                        waited[key] = val
                        eng.wait_ge(sem, val)
                    ins = op.fn(eng)
                    if op.dma_sem is not None:
                        ins.then_inc(dsem[op.dma_sem], 16)
                    elif op.signal:
                        ins.then_inc(esem[e], 1)
                if e == "pool":
                    for k in final_dma_keys:
                        eng.wait_ge(dsem[k], 16 * prog.dma_counts[k])

            @block.tensor
            def _(eng):
                run("pe", eng)

            @block.scalar
            def _(eng):
                run("act", eng)

            @block.vector
            def _(eng):
                run("dve", eng)

            @block.gpsimd
            def _(eng):
                run("pool", eng)

            @block.sync
            def _(eng):
                run("sp", eng)


def V(t, off, dims, p0=0, n=128):
    row = 1
    for s in t.shape[1:]:
        row *= s
    return bass.AP(t, p0 * row + off, [[row, n]] + [list(d) for d in dims])


def _consts():
    s = np.arange(128)[:, None]
    t = np.arange(128)[None, :]
    scmask = ((s // 32 == t // 32) & (s <= t)).astype(np.float32)
    bands = np.zeros((128, 12, 128), np.float32)
    for g, w in enumerate((2, 4, 8, 16)):
        inwin = ((t - s) >= 0) & ((t - s) < w)
        eye = (s == t).astype(np.float32)
        bands[:, g, :] = inwin * (1.0 / w) - eye
        cnt = np.minimum(t + 1, w).astype(np.float32)
        bands[:, 4 + g, :] = inwin / cnt - eye
        prev = ((t + 128 - s) < w) & (t < 16)
        bands[:, 8 + g, :] = prev * (1.0 / w)
    ident = np.eye(128, dtype=np.float32)
    ones = np.full((128, 128), 1.0 / 128.0, np.float32)
    scan = np.ones((128, 512), np.float32)
    scan[:, ::32] = 0.0
    cb = np.concatenate([ident, ones, scmask, bands.reshape(128, 12 * 128)], axis=1)
    return np.ascontiguousarray(cb), np.ascontiguousarray(scan), np.ascontiguousarray(ident)


NCB = 128 * 3 + 12 * 128


def build_nc(NT=NT_FULL, debug=None):
    nc = bass.Bass("TRN2", target_bir_lowering=False)

    def din(name, shape, dt=F32):
        return nc.dram_tensor(name, list(shape), dt, kind="ExternalInput")

    x_d = din("x", [TOK, D])
    out_d = nc.dram_tensor("out", [TOK, D], F32, kind="ExternalOutput")
    cT_d = din("cT", [128, 8, 4])
    wada_d = din("w_ada", [D, 6 * D])
    bada_d = din("b_ada", [128, 48])
    nmw_d = din("nmw", [128, 8])
    nlw_d = din("nlw", [128, 8])
    wf_d = din("wf", [128, D])
    win_d = din("w_in", [D, 2560])
    wout_d = din("w_out", [D, D])
    wup_d = din("w_up", [D, 4 * D])
    wdn_d = din("w_down", [4 * D, D])
    lbl_d = din("lbl", [128, 2, 4])
    gnw_d = din("gnw", [128, 1])
    wpool_d = din("wpool", [128, 4, 128])
    psc_d = din("psc", [128, 4])
    cb_d = din("cb", [128, NCB])
    scan_d = din("scanm", [128, 512])
    idf_d = din("idf", [128, 128])

    def dscr(name, shape):
        return nc.dram_tensor(name, list(shape), BF16, kind="Internal")

    win_s = dscr("win_s", [10, 128, 8, 256])
    wout_s = dscr("wout_s", [4, 128, 8, 256])
    wup_s = dscr("wup_s", [16, 128, 8, 256])
    wdn_s = dscr("wdn_s", [16, 128, 8, 256])

    dbg_out = {}
    if debug:
        for name, shape in debug.items():
            dbg_out[name] = nc.dram_tensor("dbg_" + name, list(shape), F32, kind="ExternalOutput")

    with ExitStack() as st:
        def sb(name, shape, dt):
            return st.enter_context(nc.sbuf_tensor(name, list(shape), dt))

        wsl = [sb("wsl%d" % i, [128, 8, 256], BF16) for i in range(NW)]
        xh = sb("xh", [128, NXS, D], F32)
        xn = sb("xn", [128, 4, D], BF16)
        ucT = sb("ucT", [128, 8, TT], BF16)
        u2T = sb("u2T", [128, 8, TT], BF16)
        th = sb("th", [128, TT], F32)
        kk32 = sb("kk32", [128, TT], F32)
        fg = sb("fg", [128, TT], F32)
        bb = sb("bb", [128, TT], F32)
        dd = sb("dd", [128, TT], F32)
        qs32 = th
        Et = [sb("E%d" % i, [128, TT], F32) for i in range(2)]
        qtil = sb("qtil", [128, 4, TT], BF16)
        ktil = sb("ktil", [128, 4, TT], BF16)
        qin = sb("qin", [128, 4, TT], BF16)
        koutT = sb("koutT", [128, 4, TT], BF16)
        sg = sb("sg", [128, 4, TT], BF16)
        vtok = sb("vtok", [128, 4, TT], BF16)
        ptok = sb("ptok", [128, 5, TT], BF16)
        decs = sb("decs", [128, 4, 16], F32)
        scTm = [sb("scTm%d" % i, [128, 4, 128], BF16) for i in range(4)]
        koutk = [sb("koutk%d" % i, [128, 4, 128], BF16) for i in range(4)]
        S32 = sb("S32", [128, 4, 128], F32)
        NSB = 9
        Sbf = [sb("Sbf%d" % i, [128, 4, 128], BF16) for i in range(NSB)]
        osq = [sb("osq0", [128, TT], BF16)] * 2
        rsb = [sb("rsb0", [128, TT], F32)] * 2
        pooledT = sb("pooledT", [128, 4, TT], BF16)
        mg = [sb("mg%d" % i, [128, TT], F32) for i in range(4)]
        mlg = mg
        hidT = sb("hidT", [128, 32, TT], BF16)
        r32 = [sb("r32_%d" % i, [128, TT], F32) for i in range(2)]
        cbt = sb("cbt", [128, NCB], BF16)
        scanm = sb("scanm_t", [128, 512], F32)
        idf = sb("idf_t", [128, 128], F32)
        wpool = sb("wpool_t", [128, 4, 128], BF16)
        wfb = sb("wfb", [128, D], F32)
        cT = sb("cT_t", [128, 8, 4], F32)
        cact = sb("cact", [128, 8, 4], BF16)
        bada = sb("bada_t", [128, 48], F32)
        modT = sb("modT", [128, 48, 4], F32)
        A1 = sb("A1", [128, 8, 4], F32)
        A2 = sb("A2", [128, 8, 4], F32)
        nmw = sb("nmw_t", [128, 8], F32)
        nlw = sb("nlw_t", [128, 8], F32)
        lbl = sb("lbl_t", [128, 2, 4], F32)
        lbp = sb("lbp", [128, 4], F32)
        omlh = sb("omlh", [128, 4], F32)
        nomlh = sb("nomlh", [128, 4], F32)
        gnw = sb("gnw_t", [128, 1], F32)
        gnwc = sb("gnwc", [128, 1], F32)
        psc = sb("psc_t", [128, 4], F32)
        ss = [sb("ss%d" % i, [128, 4], F32) for i in range(3)]
        rst = [sb("rst%d" % i, [128, 4], F32) for i in range(3)]

        pb = [st.enter_context(nc.psum_tensor("pb%d" % i, [128, 512], F32)) for i in range(8)]

        IDB = lambda: V(cbt, 0, [[1, 128]])
        ONES = lambda: V(cbt, 128, [[1, 128]])
        def BAND(i, n=128):
            return V(cbt, 384 + i * 128, [[1, n]])

        P = Prog()

        class Rec:
            def __init__(self):
                self.cur = None
            def begin(self):
                self.cur = [[]]
            def end(self):
                st_ = self.cur
                if st_ and not st_[-1]:
                    st_ = st_[:-1]
                self.cur = None
                return st_
            def cut(self):
                if self.cur is not None and self.cur[-1]:
                    self.cur.append([])
            def pad(self, n):
                if self.cur is not None:
                    self.cut()
                    for _ in range(n):
                        self.cur.append([])
            def _do(self, meth, *a, **k):
                if self.cur is None:
                    return meth(*a, **k)
                self.cur[-1].append((meth, a, k, P.label))
            def pe(self, fn, reads=(), writes=()):
                self._do(P.pe, fn, list(reads), list(writes))
            def act(self, fn, reads=(), writes=()):
                self._do(P.act, fn, list(reads), list(writes))
            def dve(self, fn, reads=(), writes=()):
                self._do(P.dve, fn, list(reads), list(writes))
            def pool(self, fn, reads=(), writes=()):
                self._do(P.pool, fn, list(reads), list(writes))
            def dma(self, eng, fn, reads=(), writes=(), key=None):
                self._do(P.dma, eng, fn, list(reads), list(writes), key)

        Q = Rec()

        def run_steps(steps):
            for stp in steps:
                for (meth, a, k, lab) in stp:
                    P.label = lab
                    meth(*a, **k)

        def merge(sa, sb_, lead=0, tail=0, first_b=0):
            out = []
            na, nb = len(sa), len(sb_)
            ib = 0
            while ib < min(first_b, nb):
                out.append(sb_[ib]); ib += 1
            ia = 0
            while ia < min(lead, na):
                out.append(sa[ia]); ia += 1
            hi = max(na - tail, ia)
            ra = max(hi - ia, 1)
            rb = max(nb - ib, 1)
            a0, b0 = ia, ib
            while ia < hi or ib < nb:
                fa = (ia - a0 + 1) / ra if ia < hi else 9.0
                fb = (ib - b0 + 1) / rb if ib < nb else 9.0
                if fb <= fa:
                    out.append(sb_[ib]); ib += 1
                else:
                    out.append(sa[ia]); ia += 1
            while ia < na:
                out.append(sa[ia]); ia += 1
            return out

        B_w = [Buf("wsl%d" % i) for i in range(NW)]
        B_xh = [Buf("xh%d" % i) for i in range(NXS)]
        B_xn = [Buf("xn%d" % i) for i in range(4)]
        B_uc = [Buf("uc%d" % i) for i in range(8)]
        B_u2 = [Buf("u2%d" % i) for i in range(8)]
        B_th, B_kk, B_fg, B_bb, B_dd = (Buf(n) for n in ("th", "kk", "fg", "bb", "dd"))
        B_qs = B_th
        B_E = [Buf("E0"), Buf("E1")]
        B_qtil = [Buf("qtil%d" % i) for i in range(4)]
        B_ktil = [Buf("ktil%d" % i) for i in range(4)]
        B_qin = [Buf("qin%d" % i) for i in range(4)]
        B_koutT = [Buf("koutT%d" % i) for i in range(4)]
        B_sg = [Buf("sg%d" % i) for i in range(4)]
        B_v = [Buf("v%d" % i) for i in range(4)]
        B_p = [Buf("p%d" % i) for i in range(5)]
        B_dec = [Buf("dec%d" % i) for i in range(4)]
        B_scTm = [Buf("scTm%d" % i) for i in range(4)]
        B_koutk = [Buf("koutk%d" % i) for i in range(4)]
        B_S32 = Buf("S32")
        B_Sbf = [Buf("Sbf%d" % i) for i in range(NSB)]
        B_osq = [Buf("osq0")] * 2
        B_rsb = [Buf("rsb0")] * 2
        B_pooled = [Buf("pooled%d" % i) for i in range(4)]
        B_mg = [Buf("mg%d" % i) for i in range(4)]
        B_mlg = B_mg
        B_hid = [Buf("hid%d" % i) for i in range(32)]
        B_r32 = [Buf("r32_0"), Buf("r32_1")]
        B_const = Buf("const")
        B_eps = Buf("eps")
        B_mod = Buf("mod")
        B_ss = [Buf("ss%d" % i) for i in range(3)]
        B_rst = [Buf("rst%d" % i) for i in range(3)]
        B_pb = [Buf("pb%d" % i, excl=True) for i in range(8)]
        B_scr = {k_: [Buf("%s_s%d" % (k_, i_)) for i_ in range(n_)] for k_, n_ in (("win", 10), ("wout", 4), ("wup", 16), ("wdn", 16))}

        state = {"wctr": 0, "wctr2": 0, "mixb": 0, "mlpb": 0, "ssr": 0, "sbf": 0, "r2": 0, "dsb": 0}

        def mix_bank():
            i = 4 + (state["mixb"] % 4)
            state["mixb"] += 1
            return i

        def mlp_bank():
            i = state["mlpb"] % 4
            state["mlpb"] += 1
            return i

        def pbf(i, off, dims):
            a = pb[i][:].bitcast(BF16)
            return bass.AP(a.tensor, a.offset + off, [list(a.ap[0])] + [list(d) for d in dims])

        def c_dma(eng, out, in_):
            Q.dma(eng, lambda e: e.dma_start(out=out, in_=in_), writes=[B_const], key="const_" + eng)

        c_dma("pool", cbt[:], cb_d.ap())
        c_dma("pool", wpool[:], wpool_d.ap())
        for (t_, d_) in ((scanm, scan_d), (idf, idf_d), (wfb, wf_d), (cT, cT_d), (bada, bada_d), (nmw, nmw_d),
                         (nlw, nlw_d), (lbl, lbl_d), (gnw, gnw_d), (psc, psc_d)):
            c_dma("sp", t_[:], d_.ap())

        def x_load(t, j):
            s = (4 * t + j) % NXS
            r0 = t * TT + j * 128
            Q.dma("pool", lambda e: e.dma_start(out=V(xh, s * D, [[1, D]]), in_=x_d.ap()[r0:r0 + 128, :]),
                  writes=[B_xh[s]], key="x%d" % s)

        for j in range(4):
            x_load(0, j)

        def conv(src, dst, npieces, kind, rows_per_piece_k0=None):
            for pc in range(npieces):
                if kind == "wdn":
                    mp, kq = pc // 4, pc % 4
                    in_ap = src.ap()[kq * 1024:(kq + 1) * 1024, mp * 256:(mp + 1) * 256].rearrange(
                        "(k p) c -> p k c", p=128)
                else:
                    in_ap = src.ap()[:, pc * 256:(pc + 1) * 256].rearrange("(k p) c -> p k c", p=128)
                Q.dma("pool", (lambda o, i: (lambda e: e.dma_start(out=o, in_=i)))(dst.ap()[pc], in_ap),
                      writes=[B_scr[kind][pc]], key="cv_%s_%d" % (kind, pc))

        Q.act(lambda e: e.activation(out=cact[:], in_=cT[:], func=AF.Silu), reads=[B_const], writes=[B_mod])
        modbank = 0

        def mod_part(hb0, hb1):
            for hb in range(hb0, hb1):
                so = (hb % 4) * 8
                sbufs = B_hid[so:so + 8]
                Q.dma("pool", (lambda so, c0: (lambda e: e.dma_start(
                    out=V(hidT, so * TT, [[TT, 8], [1, 512]]),
                    in_=wada_d.ap()[:, c0:c0 + 512].rearrange("(k p) c -> p k c", p=128))))(so, hb * 512),
                    writes=sbufs, key="stg%d" % (hb % 4))
                for nn in range(4):
                    n = hb * 4 + nn
                    for k in range(8):
                        Q.pe((lambda so, nn, n, k: (lambda e: e.matmul(
                            V(pb[modbank], n * 4, [[1, 4]]), lhsT=V(hidT, (so + k) * TT + nn * 128, [[1, 128]]),
                            rhs=V(cact, k * 4, [[1, 4]]), start=(k == 0), stop=(k == 7), skip_group_check=True)))(so, nn, n, k),
                            reads=sbufs + [B_mod], writes=[B_pb[modbank]])
            n0, n1 = hb0 * 4, hb1 * 4
            Q.dve((lambda n0, n1: (lambda e: e.tensor_tensor(
                out=V(modT, n0 * 4, [[4, n1 - n0], [1, 4]]), in0=V(pb[modbank], n0 * 4, [[4, n1 - n0], [1, 4]]),
                in1=V(bada, n0, [[1, n1 - n0], [0, 4]]), op=ALU.add)))(n0, n1),
                reads=[B_pb[modbank], B_const], writes=[B_mod])

        def mod_A(Ax, nw_, base):
            Q.dve((lambda Ax, base: (lambda e: e.tensor_scalar(
                out=V(Ax, 0, [[1, 32]]), in0=V(modT, base * 4, [[1, 32]]), scalar1=1.0, scalar2=None, op0=ALU.add)))(Ax, base),
                reads=[B_mod], writes=[B_mod])
            Q.dve((lambda Ax, nw_: (lambda e: e.tensor_tensor(
                out=V(Ax, 0, [[4, 8], [1, 4]]), in0=V(Ax, 0, [[4, 8], [1, 4]]),
                in1=V(nw_, 0, [[1, 8], [0, 4]]), op=ALU.mult)))(Ax, nw_),
                reads=[B_mod, B_const], writes=[B_mod])

        mod_part(0, 4)
        mod_A(A1, nmw, 8)
        conv(win_d, win_s, 10, "win")
        Q.dve(lambda e: e.tensor_tensor(out=lbp[:], in0=V(lbl, 0, [[1, 4]]), in1=V(lbl, 4, [[1, 4]]), op=ALU.subtract),
              reads=[B_const], writes=[B_mod])
        Q.act(lambda e: e.activation(out=lbp[:], in_=lbp[:], func=AF.Tanh, scale=0.5), reads=[B_mod], writes=[B_mod])
        Q.dve(lambda e: e.tensor_scalar(out=omlh[:], in0=lbp[:], scalar1=-0.25, scalar2=0.25, op0=ALU.mult, op1=ALU.add),
              reads=[B_mod], writes=[B_mod])
        Q.dve(lambda e: e.tensor_scalar(out=nomlh[:], in0=lbp[:], scalar1=0.25, scalar2=-0.25, op0=ALU.mult, op1=ALU.add),
              reads=[B_mod], writes=[B_mod])
        Q.dve(lambda e: e.tensor_scalar(out=lbp[:], in0=lbp[:], scalar1=0.25, scalar2=0.75, op0=ALU.mult, op1=ALU.add),
              reads=[B_mod], writes=[B_mod])
        Q.dve(lambda e: e.tensor_scalar(out=gnwc[:], in0=gnw[:], scalar1=CQ, scalar2=None, op0=ALU.mult),
              reads=[B_const], writes=[B_mod])


        MIX_SEQ = [("win", 2), ("win", 0), ("win", 3), ("win", 1), ("win", 6), ("win", 7), ("win", 4), ("win", 5),
                   ("win", 8), ("win", 9), ("wout", 0), ("wout", 1), ("wout", 2), ("wout", 3)]
        MLP_SEQ = [("wup", i) for i in range(16)] + [("wdn", i) for i in range(16)]
        W_SRC = {"win": win_s, "wout": wout_s, "wup": wup_s, "wdn": wdn_s}

        class WRing:
            def __init__(self, slots, seq, total):
                self.free = list(slots)
                self.seq = seq
                self.total = total
                self.next_load = 0
                self.next_acq = 0
                self.loaded = {}

            def _issue(self):
                if self.next_load >= self.total or not self.free:
                    return
                i = self.free.pop(0)
                kind, pc = self.seq[self.next_load % len(self.seq)]
                src = W_SRC[kind]
                Q.dma("sp", (lambda i, pc, src: (lambda e: e.dma_start(out=wsl[i][:], in_=src.ap()[pc])))(i, pc, src),
                      reads=[B_scr[kind][pc]], writes=[B_w[i]], key="w%d" % i)
                self.loaded[self.next_load] = i
                self.next_load += 1

            def prime(self):
                while self.free and self.next_load < self.total:
                    self._issue()

            def acquire(self, kind, pc):
                n = self.next_acq
                self.next_acq += 1
                assert self.seq[n % len(self.seq)] == (kind, pc), (n, kind, pc)
                if n not in self.loaded:
                    self._issue()
                return self.loaded.pop(n)

            def release(self, i):
                self.free.append(i)
                self._issue()

        ring_mix = WRing([0, 1, 2], MIX_SEQ, NT * len(MIX_SEQ))
        ring_mlp = WRing([3, 4, 5], MLP_SEQ, NT * len(MLP_SEQ))

        def WS(i, k, off, n):
            return V(wsl[i], k * 256 + off, [[1, n]])

        def norm_stats(slots, which):
            if which == 2:
                r = 2
            else:
                r = state["ssr"] % 2
                state["ssr"] += 1
            for j, s in enumerate(slots):
                if which == 2:
                    dump, Bd = r32[0][:].bitcast(BF16), B_r32[0]
                else:
                    dump, Bd = V(xn, j * D, [[1, D]]), B_xn[j]
                Q.act((lambda s, j, dump: (lambda e: e.activation(out=dump, in_=V(xh, s * D, [[1, D]]), func=AF.Square,
                                                                  accum_out=V(ss[r], j, [[1, 1]]))))(s, j, dump),
                      reads=[B_xh[s]], writes=[B_ss[r], Bd])
            Q.act(lambda e: e.activation(out=rst[r][:], in_=ss[r][:], func=AF.Ln, bias=EPSC(), scale=1.0 / D),
                  reads=[B_ss[r], B_eps], writes=[B_rst[r]])
            Q.act(lambda e: e.activation(out=rst[r][:], in_=rst[r][:], func=AF.Exp, scale=-0.5),
                  reads=[B_rst[r]], writes=[B_rst[r]])
            return r

        epsc = sb("epsc", [128, 1], F32)
        Q.pool(lambda e: e.memset(epsc[:], EPS), writes=[B_eps])
        EPSC = lambda: epsc[:]

        def norm_to_T(t, slots, Ax, shbase, dstT, B_dst):
            b = t // 4
            r = norm_stats(slots, 0)
            for j, s in enumerate(slots):
                Q.dve((lambda s, j: (lambda e: e.tensor_scalar(
                    out=V(xn, j * D, [[1, D]]), in0=V(xh, s * D, [[1, D]]), scalar1=V(rst[r], j, [[1, 1]]),
                    scalar2=None, op0=ALU.mult)))(s, j),
                    reads=[B_xh[s], B_rst[r]], writes=[B_xn[j]])
            Q.pad(3)
            for kp in range(4):
                bk = mix_bank()
                for j in range(4):
                    for kk_ in range(2):
                        k = 2 * kp + kk_
                        Q.pe((lambda j, kk_, k, bk: (lambda e: e.transpose(
                            pbf(bk, (kk_ * 4 + j) * 128, [[1, 128]]), V(xn, j * D + k * 128, [[1, 128]]), IDB())))(j, kk_, k, bk),
                            reads=[B_xn[j], B_const], writes=[B_pb[bk]])
                for kk_ in range(2):
                    k = 2 * kp + kk_
                    if kk_ == 0:
                        Q.act((lambda k, kk_, bk: (lambda e: e.activation(
                            out=V(dstT, k * TT, [[1, TT]]), in_=pbf(bk, kk_ * 512, [[1, 512]]), func=AF.Identity,
                            bias=V(modT, (shbase + k) * 4 + b, [[1, 1]]), scale=V(Ax, k * 4 + b, [[1, 1]]))))(k, kk_, bk),
                            reads=[B_pb[bk], B_mod], writes=[B_dst[k]])
                    else:
                        Q.dve((lambda k, kk_, bk: (lambda e: e.tensor_scalar(
                            out=V(dstT, k * TT, [[1, TT]]), in0=pbf(bk, kk_ * 512, [[1, 512]]),
                            scalar1=V(Ax, k * 4 + b, [[1, 1]]), scalar2=V(modT, (shbase + k) * 4 + b, [[1, 1]]),
                            op0=ALU.mult, op1=ALU.add)))(k, kk_, bk),
                            reads=[B_pb[bk], B_mod], writes=[B_dst[k]])
                Q.cut()

        def fm_chunk(wi, off, srcT, B_src, bk):
            for k in range(8):
                Q.pe((lambda k: (lambda e: e.matmul(pb[bk][:], lhsT=WS(wi, k, off, 128), rhs=V(srcT, k * TT, [[1, TT]]),
                                                    start=(k == 0), stop=(k == 7))))(k),
                     reads=[B_w[wi], B_src[k]], writes=[B_pb[bk]])

        def stage_N1(t):
            P.label = "N1%d" % t
            slots = [(4 * t + j) % NXS for j in range(4)]
            norm_to_T(t, slots, A1, 0, ucT, B_uc)

        def stage_A(t):
            P.label = "A%d" % t
            seq_first = (t % 4 == 0)
            wq = [None, None]
            wqq = [None, None]
            for h in range(4):
                if h % 2 == 0:
                    wq[h // 2] = ring_mix.acquire("win", 2 + h // 2)
                    wqq[h // 2] = ring_mix.acquire("win", h // 2)
                bk = mix_bank()
                fm_chunk(wq[h // 2], (h % 2) * 128, ucT, B_uc, bk)
                if h % 2 == 1:
                    ring_mix.release(wq[h // 2])
                Q.act((lambda bk: (lambda e: e.activation(out=th[:], in_=pb[bk][:], func=AF.Tanh, scale=0.5)))(bk),
                      reads=[B_pb[bk]], writes=[B_th])
                Q.dve((lambda h: (lambda e: e.tensor_scalar(out=kk32[:], in0=th[:], scalar1=V(nomlh, h, [[1, 1]]),
                                                            scalar2=V(omlh, h, [[1, 1]]), op0=ALU.mult, op1=ALU.add)))(h),
                      reads=[B_th, B_mod], writes=[B_kk])
                Q.dve((lambda h: (lambda e: e.tensor_scalar(out=fg[:], in0=th[:], scalar1=V(omlh, h, [[1, 1]]),
                                                            scalar2=V(lbp, h, [[1, 1]]), op0=ALU.mult, op1=ALU.add)))(h),
                      reads=[B_th, B_mod], writes=[B_fg])
                Q.cut()
                bk2 = mix_bank()
                fm_chunk(wqq[h // 2], (h % 2) * 128, ucT, B_uc, bk2)
                if h % 2 == 1:
                    ring_mix.release(wqq[h // 2])
                Q.act((lambda bk2: (lambda e: e.activation(out=qs32[:], in_=pb[bk2][:], func=AF.Silu)))(bk2),
                      reads=[B_pb[bk2]], writes=[B_qs])
                Q.cut()
                Q.act(lambda e: e.activation(out=fg[:], in_=fg[:], func=AF.Ln), reads=[B_fg], writes=[B_fg])
                Q.dve(lambda e: e.tensor_tensor_scan(out=bb[:], data0=scanm[:], data1=fg[:], initial=0.0,
                                                     op0=ALU.mult, op1=ALU.add),
                      reads=[B_fg, B_const], writes=[B_bb])
                Q.dve(lambda e: e.tensor_tensor(out=V(dd, 0, [[32, 16], [1, 32]]), in0=V(bb, 0, [[32, 16], [1, 32]]),
                                                in1=V(bb, 15, [[32, 16], [0, 32]]), op=ALU.subtract),
                      reads=[B_bb], writes=[B_dd])
                Q.dve(lambda e: e.tensor_tensor(out=V(fg, 0, [[32, 16], [1, 32]]), in0=V(bb, 31, [[32, 16], [0, 32]]),
                                                 in1=V(bb, 0, [[32, 16], [1, 32]]), op=ALU.subtract),
                       reads=[B_bb, B_fg], writes=[B_fg])
                Q.cut()
                Q.act(lambda e: e.activation(out=Et[0][:], in_=dd[:], func=AF.Exp), reads=[B_dd], writes=[B_E[0]])
                Q.dve((lambda h: (lambda e: e.tensor_tensor(out=V(qtil, h * TT, [[1, TT]]), in0=qs32[:], in1=Et[0][:],
                                                            op=ALU.mult)))(h),
                      reads=[B_qs, B_E[0]], writes=[B_qtil[h]])
                Q.act(lambda e: e.activation(out=Et[1][:], in_=dd[:], func=AF.Exp, scale=-1.0), reads=[B_dd], writes=[B_E[1]])
                Q.dve((lambda h: (lambda e: e.tensor_tensor(out=V(ktil, h * TT, [[1, TT]]), in0=kk32[:], in1=Et[1][:],
                                                             op=ALU.mult)))(h),
                       reads=[B_kk, B_E[1]], writes=[B_ktil[h]])
                Q.cut()
                Q.act(lambda e: e.activation(out=Et[0][:], in_=bb[:], func=AF.Exp), reads=[B_bb], writes=[B_E[0]])
                Q.dve((lambda h: (lambda e: e.tensor_tensor(out=V(qin, h * TT, [[1, TT]]), in0=qs32[:], in1=Et[0][:],
                                                            op=ALU.mult)))(h),
                      reads=[B_qs, B_E[0]], writes=[B_qin[h]])
                Q.dve((lambda h: (lambda e: e.tensor_copy(out=V(decs, h * 16, [[1, 16]]), in_=V(Et[0], 31, [[32, 16]]))))(h),
                      reads=[B_E[0]], writes=[B_dec[h]])
                Q.act(lambda e: e.activation(out=Et[1][:], in_=fg[:], func=AF.Exp), reads=[B_fg], writes=[B_E[1]])
                Q.dve((lambda h: (lambda e: e.tensor_tensor(out=V(koutT, h * TT, [[1, TT]]), in0=kk32[:], in1=Et[1][:],
                                                             op=ALU.mult)))(h),
                       reads=[B_kk, B_E[1]], writes=[B_koutT[h]])
                Q.cut()
            wg = [None, None]
            for h in range(4):
                if h % 2 == 0:
                    wg[h // 2] = ring_mix.acquire("win", 6 + h // 2)
                bk = mix_bank()
                fm_chunk(wg[h // 2], (h % 2) * 128, ucT, B_uc, bk)
                if h % 2 == 1:
                    ring_mix.release(wg[h // 2])
                Q.act((lambda h, bk: (lambda e: e.activation(out=V(sg, h * TT, [[1, TT]]), in_=pb[bk][:], func=AF.Silu)))(h, bk),
                      reads=[B_pb[bk]], writes=[B_sg[h]])
                Q.cut()
            for (pcs, dst, Bd, nm) in ((4, vtok, B_v, "v"), (8, ptok, B_p, "p")):
                wv = [ring_mix.acquire("win", pcs), ring_mix.acquire("win", pcs + 1)]
                for j in range(4):
                    bk = mix_bank()
                    for half in range(2):
                        for k in range(8):
                            Q.pe((lambda j, half, k, bk, wvh: (lambda e: e.matmul(
                                V(pb[bk], half * 256, [[1, 256]]), lhsT=V(ucT, k * TT + j * 128, [[1, 128]]),
                                rhs=WS(wvh, k, 0, 256), start=(k == 0), stop=(k == 7))))(j, half, k, bk, wv[half]),
                                reads=[B_w[wv[half]], B_uc[k]], writes=[B_pb[bk]])
                    if nm == "v":
                        dslot, Bds = j, Bd[j]
                    else:
                        dslot = (4 * t + j) % 5
                        Bds = Bd[dslot]
                    if j % 2 == 0:
                        Q.act((lambda dslot, bk, dst: (lambda e: e.activation(
                            out=V(dst, dslot * TT, [[1, TT]]), in_=pb[bk][:], func=AF.Copy)))(dslot, bk, dst),
                            reads=[B_pb[bk]], writes=[Bds])
                    else:
                        Q.dve((lambda dslot, bk, dst: (lambda e: e.tensor_copy(
                            out=V(dst, dslot * TT, [[1, TT]]), in_=pb[bk][:])))(dslot, bk, dst),
                            reads=[B_pb[bk]], writes=[Bds])
                    if j == 3:
                        ring_mix.release(wv[0])
                        ring_mix.release(wv[1])
                    Q.cut()

        def stage_B(t):
            P.label = "B%d" % t
            seq_first = (t % 4 == 0)
            for j in range(4):
                bsc = 4 + (j % 2)
                for h in range(4):
                    Q.pe((lambda h, j, bsc: (lambda e: e.matmul(
                        V(pb[bsc], h * 128, [[1, 128]]), lhsT=V(ktil, h * TT + j * 128, [[1, 128]]),
                        rhs=V(qtil, h * TT + j * 128, [[1, 128]]), start=True, stop=True)))(h, j, bsc),
                        reads=[B_ktil[h], B_qtil[h]], writes=[B_pb[bsc]])
                Q.dve((lambda bsc, j: (lambda e: e.tensor_tensor(
                    out=V(scTm[j], 0, [[128, 4], [1, 128]]), in0=V(pb[bsc], 0, [[128, 4], [1, 128]]),
                    in1=V(cbt, 256, [[0, 4], [1, 128]]), op=ALU.mult)))(bsc, j),
                    reads=[B_pb[bsc], B_const], writes=[B_scTm[j]])
                Q.cut()
                bkt = 6 + (j % 2)
                for h in range(4):
                    Q.pe((lambda h, j, bkt: (lambda e: e.transpose(
                        pbf(bkt, h * 128, [[1, 128]]), V(koutT, h * TT + j * 128, [[1, 128]]), IDB())))(h, j, bkt),
                        reads=[B_koutT[h], B_const], writes=[B_pb[bkt]])
                Q.act((lambda bkt, j: (lambda e: e.activation(out=V(koutk[j], 0, [[1, 512]]), in_=pbf(bkt, 0, [[1, 512]]),
                                                              func=AF.Copy)))(bkt, j),
                      reads=[B_pb[bkt]], writes=[B_koutk[j]])
                Q.cut()
            for j in range(4):
                first128 = seq_first and j == 0
                bp = 4 + (j % 4)
                ps_ = (4 * t + j) % 5
                pp_ = (4 * t + j - 1) % 5
                for g in range(4):
                    Q.pe((lambda g, bp, ps_, first128: (lambda e: e.matmul(
                        V(pb[bp], g * 128, [[1, 128]]), lhsT=V(ptok, ps_ * TT + g * 128, [[1, 128]]),
                        rhs=BAND((4 + g) if first128 else g), start=True, stop=first128, skip_group_check=True)))(g, bp, ps_, first128),
                        reads=[B_p[ps_], B_const], writes=[B_pb[bp]])
                    if not first128:
                        Q.pe((lambda g, bp, pp_: (lambda e: e.matmul(
                            V(pb[bp], g * 128, [[1, 16]]), lhsT=V(ptok, pp_ * TT + g * 128, [[1, 128]]),
                            rhs=BAND(8 + g, 16), start=False, stop=True, skip_group_check=True)))(g, bp, pp_),
                            reads=[B_p[pp_], B_const], writes=[B_pb[bp]])
                Q.dve((lambda j, bp: (lambda e: e.tensor_copy(
                    out=V(pooledT, j * 128, [[TT, 4], [1, 128]]), in_=V(pb[bp], 0, [[128, 4], [1, 128]]))))(j, bp),
                    reads=[B_pb[bp]], writes=[B_pooled[j]])
                Q.cut()
            for g in range(4):
                bx = 4 + (g % 4)
                Q.pe((lambda g, bx: (lambda e: e.matmul(pb[bx][:], lhsT=V(wpool, g * 128, [[1, 128]]),
                                                        rhs=V(pooledT, g * TT, [[1, TT]]), start=True, stop=True)))(g, bx),
                     reads=B_pooled + [B_const], writes=[B_pb[bx]])
                Q.act((lambda g, bx: (lambda e: e.activation(out=V(ucT, (4 + g) * TT, [[1, TT]]), in_=pb[bx][:], func=AF.Copy,
                                                             scale=V(psc, g, [[1, 1]]))))(g, bx),
                      reads=[B_pb[bx], B_const], writes=[B_uc[4 + g]])
                Q.cut()

            svs_all = {}

            def chain(j):
                first128 = seq_first and j == 0
                if first128:
                    Q.pool(lambda e: e.memset(S32[:], 0.0), writes=[B_S32])
                    sv0 = state["sbf"] % NSB
                    Q.pool((lambda sv0: (lambda e: e.memset(Sbf[sv0][:], 0.0)))(sv0), writes=[B_Sbf[sv0]])
                svs = [state["sbf"] % NSB]
                for jj in range(4):
                    c = j * 4 + jj
                    bds = 4 + (state["dsb"] % 2)
                    state["dsb"] += 1
                    for h in range(4):
                        Q.pe((lambda h, jj, j, bds: (lambda e: e.matmul(
                            V(pb[bds], h * 128, [[1, 128]]), lhsT=V(koutk[j], h * 128, [[1, 128]], p0=32 * jj, n=32),
                            rhs=V(vtok, j * TT + h * 128, [[1, 128]], p0=32 * jj, n=32), start=True, stop=True,
                            tile_position=(32 * jj, 0))))(h, jj, j, bds),
                            reads=[B_koutk[j], B_v[j]], writes=[B_pb[bds]])
                    for h in range(4):
                        Q.dve((lambda h, c, bds: (lambda e: e.scalar_tensor_tensor(
                            out=V(S32, h * 128, [[1, 128]]), in0=V(S32, h * 128, [[1, 128]]),
                            scalar=V(decs, h * 16 + c, [[1, 1]]), in1=V(pb[bds], h * 128, [[1, 128]]),
                            op0=ALU.mult, op1=ALU.add)))(h, c, bds),
                            reads=[B_pb[bds], B_dec[h]], writes=[B_S32])
                    state["sbf"] += 1
                    sv = state["sbf"] % NSB
                    svs.append(sv)
                    Q.act((lambda sv: (lambda e: e.activation(out=V(Sbf[sv], 0, [[1, 512]]), in_=V(S32, 0, [[1, 512]]),
                                                              func=AF.Copy)))(sv),
                          reads=[B_S32], writes=[B_Sbf[sv]])
                    Q.pad(1)
                svs_all[j] = svs

            def oT(j):
                bo = 6 + (j % 2)
                svs = svs_all[j]
                for h in range(4):
                    Q.pe((lambda h, j, bo: (lambda e: e.matmul(
                        V(pb[bo], h * 128, [[1, 128]]), lhsT=V(vtok, j * TT + h * 128, [[1, 128]]),
                        rhs=V(scTm[j], h * 128, [[1, 128]]), start=True, stop=False, skip_group_check=True)))(h, j, bo),
                        reads=[B_v[j], B_scTm[j]], writes=[B_pb[bo]])
                    for jj in range(4):
                        sv = svs[jj]
                        Q.pe((lambda h, j, jj, bo, sv: (lambda e: e.matmul(
                            V(pb[bo], h * 128 + jj * 32, [[1, 32]]), lhsT=V(Sbf[sv], h * 128, [[1, 128]]),
                            rhs=V(qin, h * TT + j * 128 + jj * 32, [[1, 32]]), start=False, stop=(jj == 3),
                            skip_group_check=True)))(h, j, jj, bo, sv),
                            reads=[B_Sbf[sv], B_qin[h]], writes=[B_pb[bo]])
                Q.act((lambda bo: (lambda e: e.activation(out=osq[0][:], in_=pb[bo][:], func=AF.Square, scale=CQ)))(bo),
                      reads=[B_pb[bo]], writes=[B_osq[0]])
                Q.cut()

            def onorm(j):
                bo = 6 + (j % 2)
                bm = 4 + (state["dsb"] % 2)
                state["dsb"] += 1
                Q.pe((lambda bm: (lambda e: e.matmul(pb[bm][:], lhsT=ONES(), rhs=osq[0][:], start=True, stop=True)))(bm),
                     reads=[B_osq[0], B_const], writes=[B_pb[bm]])
                Q.act((lambda bm: (lambda e: e.activation(out=rsb[0][:], in_=pb[bm][:], func=AF.Ln, bias=EPSC())))(bm),
                      reads=[B_pb[bm], B_eps], writes=[B_rsb[0]])
                Q.act(lambda e: e.activation(out=rsb[0][:], in_=rsb[0][:], func=AF.Exp, scale=-0.5),
                      reads=[B_rsb[0]], writes=[B_rsb[0]])
                Q.dve((lambda bo: (lambda e: e.tensor_tensor(out=rsb[0][:], in0=pb[bo][:], in1=rsb[0][:], op=ALU.mult)))(bo),
                      reads=[B_pb[bo], B_rsb[0]], writes=[B_rsb[0]])
                Q.dve((lambda j: (lambda e: e.scalar_tensor_tensor(
                    out=V(ucT, j * 128, [[TT, 4], [1, 128]]), in0=V(rsb[0], 0, [[128, 4], [1, 128]]), scalar=gnwc[:],
                    in1=V(sg, j * 128, [[TT, 4], [1, 128]]), op0=ALU.mult, op1=ALU.mult)))(j),
                    reads=[B_rsb[0], B_mod] + B_sg, writes=B_uc[0:4])
                Q.cut()

            chain(0)
            chain(1)
            oT(0)
            chain(2)
            onorm(0)
            oT(1)
            chain(3)
            onorm(1)
            oT(2)
            Q.pad(2)
            onorm(2)
            oT(3)
            Q.pad(2)
            onorm(3)

        def back_transpose_add(src, B_src, m, slots, bank_fn):
            bt = bank_fn()
            for j in range(4):
                Q.pe((lambda j, bt: (lambda e: e.transpose(V(pb[bt], j * 128, [[1, 128]]), V(src, j * 128, [[1, 128]]), idf[:])))(j, bt),
                     reads=[B_src, B_const], writes=[B_pb[bt]])
            s0 = slots[0]
            Q.dve((lambda bt, s0, m: (lambda e: e.tensor_tensor(
                out=V(xh, s0 * D + m * 128, [[D, 4], [1, 128]]), in0=V(xh, s0 * D + m * 128, [[D, 4], [1, 128]]),
                in1=V(pb[bt], 0, [[128, 4], [1, 128]]), op=ALU.add)))(bt, s0, m),
                reads=[B_pb[bt]], writes=[B_xh[s] for s in slots])

        def stage_C(t):
            P.label = "C%d" % t
            b = t // 4
            slots = [(4 * t + j) % NXS for j in range(4)]
            wi = None
            for m in range(8):
                if m % 2 == 0:
                    wi = ring_mix.acquire("wout", m // 2)
                bk = mix_bank()
                fm_chunk(wi, (m % 2) * 128, ucT, B_uc, bk)
                if m % 2 == 1:
                    ring_mix.release(wi)
                r = m % 2
                Q.act((lambda bk, r, m: (lambda e: e.activation(out=mg[r][:], in_=pb[bk][:], func=AF.Copy,
                                                                scale=V(modT, (16 + m) * 4 + b, [[1, 1]]))))(bk, r, m),
                      reads=[B_pb[bk], B_mod], writes=[B_mg[r]])
                if m >= 1:
                    back_transpose_add(mg[(m - 1) % 2], B_mg[(m - 1) % 2], m - 1, slots, mix_bank)
                Q.cut()
            Q.pad(1)
            back_transpose_add(mg[1], B_mg[1], 7, slots, mix_bank)
            Q.cut()

        def stage_N2(t):
            P.label = "N2%d" % t
            slots = [(4 * t + j) % NXS for j in range(4)]
            norm_to_T(t, slots, A2, 24, u2T, B_u2)

        def stage_UP(t):
            P.label = "UP%d" % t
            wi = None
            for m in range(32):
                if m % 2 == 0:
                    wi = ring_mlp.acquire("wup", m // 2)
                bk = mlp_bank()
                fm_chunk(wi, (m % 2) * 128, u2T, B_u2, bk)
                if m % 2 == 1:
                    ring_mlp.release(wi)
                r = m % 2
                Q.act((lambda bk, r: (lambda e: e.activation(out=r32[r][:], in_=pb[bk][:], func=AF.Relu)))(bk, r),
                      reads=[B_pb[bk]], writes=[B_r32[r]])
                Q.dve((lambda bk, r, m: (lambda e: e.scalar_tensor_tensor(
                    out=V(hidT, m * TT, [[1, TT]]), in0=pb[bk][:], scalar=0.0, in1=r32[r][:],
                    op0=ALU.max, op1=ALU.mult)))(bk, r, m),
                    reads=[B_pb[bk], B_r32[r]], writes=[B_hid[m]])
                Q.cut()

        def stage_DOWN(t):
            P.label = "DOWN%d" % t
            b = t // 4
            slots = [(4 * t + j) % NXS for j in range(4)]
            pend = []
            for mp in range(4):
                bks = [mlp_bank(), mlp_bank()]
                for kq in range(4):
                    wi = ring_mlp.acquire("wdn", mp * 4 + kq)
                    for k in range(8):
                        for half in range(2):
                            Q.pe((lambda k, half, kq, wi, bkh: (lambda e: e.matmul(
                                pb[bkh][:], lhsT=WS(wi, k, half * 128, 128), rhs=V(hidT, (kq * 8 + k) * TT, [[1, TT]]),
                                start=(kq == 0 and k == 0), stop=(kq == 3 and k == 7))))(k, half, kq, wi, bks[half]),
                                reads=[B_w[wi], B_hid[kq * 8 + k]], writes=[B_pb[bks[half]]])
                    ring_mlp.release(wi)
                    Q.cut()
                if pend:
                    mq = pend.pop(0)
                    for half in range(2):
                        back_transpose_add(mg[2 + half], B_mg[2 + half], 2 * mq + half, slots, mlp_bank)
                for half in range(2):
                    m = 2 * mp + half
                    Q.act((lambda half, m, bkh: (lambda e: e.activation(out=mg[2 + half][:], in_=pb[bkh][:], func=AF.Copy,
                                                                   scale=V(modT, (40 + m) * 4 + b, [[1, 1]]))))(half, m, bks[half]),
                          reads=[B_pb[bks[half]], B_mod], writes=[B_mg[2 + half]])
                pend.append(mp)
                Q.cut()
            Q.pad(1)
            mq = pend.pop(0)
            for half in range(2):
                back_transpose_add(mg[2 + half], B_mg[2 + half], 2 * mq + half, slots, mlp_bank)
            Q.cut()

        def stage_OUT(t):
            P.label = "OUT%d" % t
            slots = [(4 * t + j) % NXS for j in range(4)]
            r = norm_stats(slots, 2)
            for j, s in enumerate(slots):
                Q.dve((lambda s, j: (lambda e: e.scalar_tensor_tensor(
                    out=V(xh, s * D, [[1, D]]), in0=V(xh, s * D, [[1, D]]), scalar=V(rst[r], j, [[1, 1]]),
                    in1=wfb[:], op0=ALU.mult, op1=ALU.mult)))(s, j),
                    reads=[B_xh[s], B_rst[r], B_const], writes=[B_xh[s]])
                r0 = t * TT + j * 128
                Q.dma("pool", (lambda s, r0: (lambda e: e.dma_start(out=out_d.ap()[r0:r0 + 128, :], in_=V(xh, s * D, [[1, D]]))))(s, r0),
                      reads=[B_xh[s]], key="o%d" % s)
                Q.cut()

        for j in range(4):
            if NT > 1:
                x_load(1, j)
        conv(wout_d, wout_s, 4, "wout")
        ring_mix.prime()
        stage_N1(0)
        stage_A(0)
        mod_part(4, 6)
        stage_B(0)
        stage_C(0)
        mod_part(6, 12)
        mod_A(A2, nlw, 32)
        conv(wup_d, wup_s, 16, "wup")
        conv(wdn_d, wdn_s, 16, "wdn")
        ring_mlp.prime()
        stage_N2(0)
        for t in range(NT):
            Q.begin()
            stage_UP(t)
            stage_DOWN(t)
            stage_OUT(t)
            s_mlp = Q.end()
            s_mix = []
            if t + 1 < NT:
                Q.begin()
                stage_N1(t + 1)
                stage_A(t + 1)
                stage_B(t + 1)
                stage_C(t + 1)
                stage_N2(t + 1)
                s_mix = Q.end()
            run_steps(merge(s_mlp, s_mix, lead=LEAD, tail=TAIL, first_b=0))
            if t + 2 < NT:
                for j in range(4):
                    x_load(t + 2, j)

        if debug:
            pass

        okeys = sorted(k for k in P.dma_counts if isinstance(k, str) and k.startswith("o"))
        P.emit(nc, final_dma_keys=okeys)
        build_nc.labels = {e: [o.label for o in P.ops[e]] for e in ENGS}
        build_nc.stats = {e: (len(P.ops[e]), sum(1 for o in P.ops[e] if o.signal)) for e in ENGS}
    return nc


def make_in_maps(inputs, ncores=NCORES):
    f = lambda a: np.ascontiguousarray(np.asarray(a, dtype=np.float32))
    x = f(inputs["x"])
    c = f(inputs["c"])
    cb, scan, ident = _consts()
    shared = {
        "w_ada": f(inputs["w_ada"][0]),
        "b_ada": f(inputs["b_ada"][0].reshape(48, 128).T),
        "nmw": f(inputs["norm_mix_w"][0].reshape(8, 128).T),
        "nlw": f(inputs["norm_mlp_w"][0].reshape(8, 128).T),
        "wf": f(np.broadcast_to(np.asarray(inputs["norm_final_w"], np.float32)[None, :], (128, D))),
        "w_in": f(inputs["w_in"][0]),
        "w_out": f(inputs["w_out"][0]),
        "w_up": f(inputs["w_up"][0]),
        "w_down": f(inputs["w_down"][0]),
        "lbl": f(np.asarray(inputs["lb_logits"], np.float32).reshape(2, 4, 128).transpose(2, 0, 1)),
        "gnw": f(np.asarray(inputs["g_norm_w"], np.float32)[0].reshape(128, 1)),
        "wpool": f(np.asarray(inputs["w_pool"], np.float32)[0].transpose(1, 0, 2)),
        "psc": f(np.asarray(inputs["pool_scale"], np.float32)[0].reshape(4, 128).T),
        "cb": cb, "scanm": scan, "idf": ident,
    }
    maps = []
    for i in range(ncores):
        m = dict(shared)
        m["x"] = np.ascontiguousarray(x[4 * i:4 * i + 4].reshape(TOK, D))
        m["cT"] = f(c[4 * i:4 * i + 4].T.reshape(8, 128, 4).transpose(1, 0, 2))
        maps.append(m)
    return maps


_NC_CACHE = {}


def kernel(**inputs):
    if "nc" not in _NC_CACHE:
        _NC_CACHE["nc"] = build_nc()
    nc = _NC_CACHE["nc"]
    maps = make_in_maps(inputs)
    res = run_bass_kernel_spmd(nc, maps, core_ids=list(range(NCORES)))
    outs = [np.asarray(r["out"], dtype=np.float32).reshape(4, SEQ, D) for r in res.results]
    return np.concatenate(outs, axis=0)
```

```python
import numpy as np
from contextlib import ExitStack
import concourse.bass as bass
import concourse.mybir as mybir
from concourse.bass_utils import run_bass_kernel_spmd

F32 = mybir.dt.float32
BF16 = mybir.dt.bfloat16
AF = mybir.ActivationFunctionType
ALU = mybir.AluOpType

ENGS = ("pe", "act", "dve", "pool", "sp")
EPS = 1e-6
NCORES = 8
SEQ = 2048
D = 1024
TOK = 4 * SEQ
TT = 512
NT_FULL = TOK // TT
NW = 6
NXS = 8
CQ = float(128 ** -0.5)
LEAD = 9
TAIL = 2


class Buf:
    __slots__ = ("name", "w", "r", "excl", "multi")

    def __init__(self, name, excl=False, multi=False):
        self.name = name
        self.w = None
        self.r = []
        self.excl = excl
        self.multi = multi


class Op:
    __slots__ = ("eng", "fn", "waits", "signal", "cnt", "dma_sem", "dma_val", "label")

    def __init__(self, eng, fn):
        self.eng = eng
        self.fn = fn
        self.waits = []
        self.signal = False
        self.cnt = 0
        self.dma_sem = None
        self.dma_val = 0


class Prog:
    def __init__(self):
        self.ops = {e: [] for e in ENGS}
        self.dma_counts = {}
        self.label = ""


    def _add(self, eng, fn, reads, writes, dma_key=None):
        op = Op(eng, fn)
        op.label = self.label
        lst = self.ops[eng]
        idx = len(lst)
        deps = []
        for b in reads:
            if b.excl:
                continue
            if b.w is not None:
                deps.append(b.w)
        wr = list(writes) + [b for b in reads if b.excl]
        for b in wr:
            if b.w is not None and not (b.multi and dma_key is not None):
                deps.append(b.w)
            deps.extend(b.r)
        if dma_key is not None:
            n = self.dma_counts.get(dma_key, 0) + 1
            self.dma_counts[dma_key] = n
            op.dma_sem = dma_key
            op.dma_val = 16 * n
            ev = ("dma", dma_key, 16 * n)
        else:
            ev = ("eng", eng, idx)
        best = {}
        for d in deps:
            k = (d[0], d[1])
            if k not in best or d[2] > best[k][2]:
                best[k] = d
        for k, d in best.items():
            if d[0] == "eng" and d[1] == eng and eng == "pe":
                continue
            op.waits.append(d)
        for b in reads:
            if not b.excl:
                b.r = [r for r in b.r if (r[0], r[1]) != (ev[0], ev[1])] + [ev]
        for b in wr:
            b.w = ev
            b.r = []
        lst.append(op)
        return op

    def pe(self, fn, reads=(), writes=()):
        return self._add("pe", fn, reads, writes)

    def act(self, fn, reads=(), writes=()):
        return self._add("act", fn, reads, writes)

    def dve(self, fn, reads=(), writes=()):
        return self._add("dve", fn, reads, writes)

    def pool(self, fn, reads=(), writes=()):
        return self._add("pool", fn, reads, writes)

    def dma(self, eng, fn, reads=(), writes=(), key=None):
        return self._add(eng, fn, reads, writes, dma_key=key)

    def emit(self, nc, final_dma_keys=()):
        for e in ENGS:
            for op in self.ops[e]:
                for d in op.waits:
                    if d[0] == "eng":
                        self.ops[d[1]][d[2]].signal = True
        for e in ENGS:
            c = 0
            for op in self.ops[e]:
                if op.signal:
                    c += 1
                op.cnt = c
        dma_keys = sorted(self.dma_counts.keys())
        with ExitStack() as st:
            esem = {e: st.enter_context(nc.semaphore("s_" + e)) for e in ENGS}
            dsem = {k: st.enter_context(nc.semaphore("d_" + str(k))) for k in dma_keys}
            block = st.enter_context(nc.Block())
            prog = self

            def run(e, eng):
                waited = {}
                for op in prog.ops[e]:
                    for d in op.waits:
                        if d[0] == "eng":
                            tgt = prog.ops[d[1]][d[2]]
                            key, val, sem = ("e", d[1]), tgt.cnt, esem[d[1]]
                        else:
                            key, val, sem = ("d", d[1]), d[2], dsem[d[1]]
                        if waited.get(key, 0) >= val:
                            continue
                        waited[key] = val
                        eng.wait_ge(sem, val)
                    ins = op.fn(eng)
                    if op.dma_sem is not None:
                        ins.then_inc(dsem[op.dma_sem], 16)
                    elif op.signal:
                        ins.then_inc(esem[e], 1)
                if e == "pool":
                    for k in final_dma_keys:
                        eng.wait_ge(dsem[k], 16 * prog.dma_counts[k])

            @block.tensor
            def _(eng):
                run("pe", eng)

            @block.scalar
            def _(eng):
                run("act", eng)

            @block.vector
            def _(eng):
                run("dve", eng)

            @block.gpsimd
            def _(eng):
                run("pool", eng)

            @block.sync
            def _(eng):
                run("sp", eng)


def V(t, off, dims, p0=0, n=128):
    row = 1
    for s in t.shape[1:]:
        row *= s
    return bass.AP(t, p0 * row + off, [[row, n]] + [list(d) for d in dims])


def _consts():
    s = np.arange(128)[:, None]
    t = np.arange(128)[None, :]
    scmask = ((s // 32 == t // 32) & (s <= t)).astype(np.float32)
    bands = np.zeros((128, 12, 128), np.float32)
    for g, w in enumerate((2, 4, 8, 16)):
        inwin = ((t - s) >= 0) & ((t - s) < w)
        eye = (s == t).astype(np.float32)
        bands[:, g, :] = inwin * (1.0 / w) - eye
        cnt = np.minimum(t + 1, w).astype(np.float32)
        bands[:, 4 + g, :] = inwin / cnt - eye
        prev = ((t + 128 - s) < w) & (t < 16)
        bands[:, 8 + g, :] = prev * (1.0 / w)
    ident = np.eye(128, dtype=np.float32)
    ones = np.full((128, 128), 1.0 / 128.0, np.float32)
    scan = np.ones((128, 512), np.float32)
    scan[:, ::32] = 0.0
    cb = np.concatenate([ident, ones, scmask, bands.reshape(128, 12 * 128)], axis=1)
    return np.ascontiguousarray(cb), np.ascontiguousarray(scan), np.ascontiguousarray(ident)


NCB = 128 * 3 + 12 * 128


def build_nc(NT=NT_FULL, debug=None):
    nc = bass.Bass("TRN2", target_bir_lowering=False)

    def din(name, shape, dt=F32):
        return nc.dram_tensor(name, list(shape), dt, kind="ExternalInput")

    x_d = din("x", [TOK, D])
    out_d = nc.dram_tensor("out", [TOK, D], F32, kind="ExternalOutput")
    cT_d = din("cT", [128, 8, 4])
    wada_d = din("w_ada", [D, 6 * D])
    bada_d = din("b_ada", [128, 48])
    nmw_d = din("nmw", [128, 8])
    nlw_d = din("nlw", [128, 8])
    wf_d = din("wf", [128, D])
    win_d = din("w_in", [D, 2560])
    wout_d = din("w_out", [D, D])
    wup_d = din("w_up", [D, 4 * D])
    wdn_d = din("w_down", [4 * D, D])
    lbl_d = din("lbl", [128, 2, 4])
    gnw_d = din("gnw", [128, 1])
    wpool_d = din("wpool", [128, 4, 128])
    psc_d = din("psc", [128, 4])
    cb_d = din("cb", [128, NCB])
    scan_d = din("scanm", [128, 512])
    idf_d = din("idf", [128, 128])

    def dscr(name, shape):
        return nc.dram_tensor(name, list(shape), BF16, kind="Internal")

    win_s = dscr("win_s", [10, 128, 8, 256])
    wout_s = dscr("wout_s", [4, 128, 8, 256])
    wup_s = dscr("wup_s", [16, 128, 8, 256])
    wdn_s = dscr("wdn_s", [16, 128, 8, 256])

    dbg_out = {}
    if debug:
        for name, shape in debug.items():
            dbg_out[name] = nc.dram_tensor("dbg_" + name, list(shape), F32, kind="ExternalOutput")

    with ExitStack() as st:
        def sb(name, shape, dt):
            return st.enter_context(nc.sbuf_tensor(name, list(shape), dt))

        wsl = [sb("wsl%d" % i, [128, 8, 256], BF16) for i in range(NW)]
        xh = sb("xh", [128, NXS, D], F32)
        xn = sb("xn", [128, 4, D], BF16)
        ucT = sb("ucT", [128, 8, TT], BF16)
        u2T = sb("u2T", [128, 8, TT], BF16)
        th = sb("th", [128, TT], F32)
        kk32 = sb("kk32", [128, TT], F32)
        fg = sb("fg", [128, TT], F32)
        bb = sb("bb", [128, TT], F32)
        dd = sb("dd", [128, TT], F32)
        qs32 = th
        Et = [sb("E%d" % i, [128, TT], F32) for i in range(2)]
        qtil = sb("qtil", [128, 4, TT], BF16)
        ktil = sb("ktil", [128, 4, TT], BF16)
        qin = sb("qin", [128, 4, TT], BF16)
        koutT = sb("koutT", [128, 4, TT], BF16)
        sg = sb("sg", [128, 4, TT], BF16)
        vtok = sb("vtok", [128, 4, TT], BF16)
        ptok = sb("ptok", [128, 5, TT], BF16)
        decs = sb("decs", [128, 4, 16], F32)
        scTm = [sb("scTm%d" % i, [128, 4, 128], BF16) for i in range(4)]
        koutk = [sb("koutk%d" % i, [128, 4, 128], BF16) for i in range(4)]
        S32 = sb("S32", [128, 4, 128], F32)
        NSB = 9
        Sbf = [sb("Sbf%d" % i, [128, 4, 128], BF16) for i in range(NSB)]
        osq = [sb("osq0", [128, TT], BF16)] * 2
        rsb = [sb("rsb0", [128, TT], F32)] * 2
        pooledT = sb("pooledT", [128, 4, TT], BF16)
        mg = [sb("mg%d" % i, [128, TT], F32) for i in range(4)]
        mlg = mg
        hidT = sb("hidT", [128, 32, TT], BF16)
        r32 = [sb("r32_%d" % i, [128, TT], F32) for i in range(2)]
        cbt = sb("cbt", [128, NCB], BF16)
        scanm = sb("scanm_t", [128, 512], F32)
        idf = sb("idf_t", [128, 128], F32)
        wpool = sb("wpool_t", [128, 4, 128], BF16)
        wfb = sb("wfb", [128, D], F32)
        cT = sb("cT_t", [128, 8, 4], F32)
        cact = sb("cact", [128, 8, 4], BF16)
        bada = sb("bada_t", [128, 48], F32)
        modT = sb("modT", [128, 48, 4], F32)
        A1 = sb("A1", [128, 8, 4], F32)
        A2 = sb("A2", [128, 8, 4], F32)
        nmw = sb("nmw_t", [128, 8], F32)
        nlw = sb("nlw_t", [128, 8], F32)
        lbl = sb("lbl_t", [128, 2, 4], F32)
        lbp = sb("lbp", [128, 4], F32)
        omlh = sb("omlh", [128, 4], F32)
        nomlh = sb("nomlh", [128, 4], F32)
        gnw = sb("gnw_t", [128, 1], F32)
        gnwc = sb("gnwc", [128, 1], F32)
        psc = sb("psc_t", [128, 4], F32)
        ss = [sb("ss%d" % i, [128, 4], F32) for i in range(3)]
        rst = [sb("rst%d" % i, [128, 4], F32) for i in range(3)]

        pb = [st.enter_context(nc.psum_tensor("pb%d" % i, [128, 512], F32)) for i in range(8)]

        IDB = lambda: V(cbt, 0, [[1, 128]])
        ONES = lambda: V(cbt, 128, [[1, 128]])
        def BAND(i, n=128):
            return V(cbt, 384 + i * 128, [[1, n]])

        P = Prog()

        class Rec:
            def __init__(self):
                self.cur = None
            def begin(self):
                self.cur = [[]]
            def end(self):
                st_ = self.cur
                if st_ and not st_[-1]:
                    st_ = st_[:-1]
                self.cur = None
                return st_
            def cut(self):
                if self.cur is not None and self.cur[-1]:
                    self.cur.append([])
            def pad(self, n):
                if self.cur is not None:
                    self.cut()
                    for _ in range(n):
                        self.cur.append([])
            def _do(self, meth, *a, **k):
                if self.cur is None:
                    return meth(*a, **k)
                self.cur[-1].append((meth, a, k, P.label))
            def pe(self, fn, reads=(), writes=()):
                self._do(P.pe, fn, list(reads), list(writes))
            def act(self, fn, reads=(), writes=()):
                self._do(P.act, fn, list(reads), list(writes))
            def dve(self, fn, reads=(), writes=()):
                self._do(P.dve, fn, list(reads), list(writes))
            def pool(self, fn, reads=(), writes=()):
                self._do(P.pool, fn, list(reads), list(writes))
            def dma(self, eng, fn, reads=(), writes=(), key=None):
                self._do(P.dma, eng, fn, list(reads), list(writes), key)

        Q = Rec()

        def run_steps(steps):
            for stp in steps:
                for (meth, a, k, lab) in stp:
                    P.label = lab
                    meth(*a, **k)

        def merge(sa, sb_, lead=0, tail=0, first_b=0):
            out = []
            na, nb = len(sa), len(sb_)
            ib = 0
            while ib < min(first_b, nb):
                out.append(sb_[ib]); ib += 1
            ia = 0
            while ia < min(lead, na):
                out.append(sa[ia]); ia += 1
            hi = max(na - tail, ia)
            ra = max(hi - ia, 1)
            rb = max(nb - ib, 1)
            a0, b0 = ia, ib
            while ia < hi or ib < nb:
                fa = (ia - a0 + 1) / ra if ia < hi else 9.0
                fb = (ib - b0 + 1) / rb if ib < nb else 9.0
                if fb <= fa:
                    out.append(sb_[ib]); ib += 1
                else:
                    out.append(sa[ia]); ia += 1
            while ia < na:
                out.append(sa[ia]); ia += 1
            return out

        B_w = [Buf("wsl%d" % i) for i in range(NW)]
        B_xh = [Buf("xh%d" % i) for i in range(NXS)]
        B_xn = [Buf("xn%d" % i) for i in range(4)]
        B_uc = [Buf("uc%d" % i) for i in range(8)]
        B_u2 = [Buf("u2%d" % i) for i in range(8)]
        B_th, B_kk, B_fg, B_bb, B_dd = (Buf(n) for n in ("th", "kk", "fg", "bb", "dd"))
        B_qs = B_th
        B_E = [Buf("E0"), Buf("E1")]
        B_qtil = [Buf("qtil%d" % i) for i in range(4)]
        B_ktil = [Buf("ktil%d" % i) for i in range(4)]
        B_qin = [Buf("qin%d" % i) for i in range(4)]
        B_koutT = [Buf("koutT%d" % i) for i in range(4)]
        B_sg = [Buf("sg%d" % i) for i in range(4)]
        B_v = [Buf("v%d" % i) for i in range(4)]
        B_p = [Buf("p%d" % i) for i in range(5)]
        B_dec = [Buf("dec%d" % i) for i in range(4)]
        B_scTm = [Buf("scTm%d" % i) for i in range(4)]
        B_koutk = [Buf("koutk%d" % i) for i in range(4)]
        B_S32 = Buf("S32")
        B_Sbf = [Buf("Sbf%d" % i) for i in range(NSB)]
        B_osq = [Buf("osq0")] * 2
        B_rsb = [Buf("rsb0")] * 2
        B_pooled = [Buf("pooled%d" % i) for i in range(4)]
        B_mg = [Buf("mg%d" % i) for i in range(4)]
        B_mlg = B_mg
        B_hid = [Buf("hid%d" % i) for i in range(32)]
        B_r32 = [Buf("r32_0"), Buf("r32_1")]
        B_const = Buf("const")
        B_eps = Buf("eps")
        B_mod = Buf("mod")
        B_ss = [Buf("ss%d" % i) for i in range(3)]
        B_rst = [Buf("rst%d" % i) for i in range(3)]
        B_pb = [Buf("pb%d" % i, excl=True) for i in range(8)]
        B_scr = {k_: [Buf("%s_s%d" % (k_, i_)) for i_ in range(n_)] for k_, n_ in (("win", 10), ("wout", 4), ("wup", 16), ("wdn", 16))}

        state = {"wctr": 0, "wctr2": 0, "mixb": 0, "mlpb": 0, "ssr": 0, "sbf": 0, "r2": 0, "dsb": 0}

        def mix_bank():
            i = 4 + (state["mixb"] % 4)
            state["mixb"] += 1
            return i

        def mlp_bank():
            i = state["mlpb"] % 4
            state["mlpb"] += 1
            return i

        def pbf(i, off, dims):
            a = pb[i][:].bitcast(BF16)
            return bass.AP(a.tensor, a.offset + off, [list(a.ap[0])] + [list(d) for d in dims])

        def c_dma(eng, out, in_):
            Q.dma(eng, lambda e: e.dma_start(out=out, in_=in_), writes=[B_const], key="const_" + eng)

        c_dma("pool", cbt[:], cb_d.ap())
        c_dma("pool", wpool[:], wpool_d.ap())
        for (t_, d_) in ((scanm, scan_d), (idf, idf_d), (wfb, wf_d), (cT, cT_d), (bada, bada_d), (nmw, nmw_d),
                         (nlw, nlw_d), (lbl, lbl_d), (gnw, gnw_d), (psc, psc_d)):
            c_dma("sp", t_[:], d_.ap())

        def x_load(t, j):
            s = (4 * t + j) % NXS
            r0 = t * TT + j * 128
            Q.dma("pool", lambda e: e.dma_start(out=V(xh, s * D, [[1, D]]), in_=x_d.ap()[r0:r0 + 128, :]),
                  writes=[B_xh[s]], key="x%d" % s)

        for j in range(4):
            x_load(0, j)

        def conv(src, dst, npieces, kind, rows_per_piece_k0=None):
            for pc in range(npieces):
                if kind == "wdn":
                    mp, kq = pc // 4, pc % 4
                    in_ap = src.ap()[kq * 1024:(kq + 1) * 1024, mp * 256:(mp + 1) * 256].rearrange(
                        "(k p) c -> p k c", p=128)
                else:
                    in_ap = src.ap()[:, pc * 256:(pc + 1) * 256].rearrange("(k p) c -> p k c", p=128)
                Q.dma("pool", (lambda o, i: (lambda e: e.dma_start(out=o, in_=i)))(dst.ap()[pc], in_ap),
                      writes=[B_scr[kind][pc]], key="cv_%s_%d" % (kind, pc))

        Q.act(lambda e: e.activation(out=cact[:], in_=cT[:], func=AF.Silu), reads=[B_const], writes=[B_mod])
        modbank = 0

        def mod_part(hb0, hb1):
            for hb in range(hb0, hb1):
                so = (hb % 4) * 8
                sbufs = B_hid[so:so + 8]
                Q.dma("pool", (lambda so, c0: (lambda e: e.dma_start(
                    out=V(hidT, so * TT, [[TT, 8], [1, 512]]),
                    in_=wada_d.ap()[:, c0:c0 + 512].rearrange("(k p) c -> p k c", p=128))))(so, hb * 512),
                    writes=sbufs, key="stg%d" % (hb % 4))
                for nn in range(4):
                    n = hb * 4 + nn
                    for k in range(8):
                        Q.pe((lambda so, nn, n, k: (lambda e: e.matmul(
                            V(pb[modbank], n * 4, [[1, 4]]), lhsT=V(hidT, (so + k) * TT + nn * 128, [[1, 128]]),
                            rhs=V(cact, k * 4, [[1, 4]]), start=(k == 0), stop=(k == 7), skip_group_check=True)))(so, nn, n, k),
                            reads=sbufs + [B_mod], writes=[B_pb[modbank]])
            n0, n1 = hb0 * 4, hb1 * 4
            Q.dve((lambda n0, n1: (lambda e: e.tensor_tensor(
                out=V(modT, n0 * 4, [[4, n1 - n0], [1, 4]]), in0=V(pb[modbank], n0 * 4, [[4, n1 - n0], [1, 4]]),
                in1=V(bada, n0, [[1, n1 - n0], [0, 4]]), op=ALU.add)))(n0, n1),
                reads=[B_pb[modbank], B_const], writes=[B_mod])

        def mod_A(Ax, nw_, base):
            Q.dve((lambda Ax, base: (lambda e: e.tensor_scalar(
                out=V(Ax, 0, [[1, 32]]), in0=V(modT, base * 4, [[1, 32]]), scalar1=1.0, scalar2=None, op0=ALU.add)))(Ax, base),
                reads=[B_mod], writes=[B_mod])
            Q.dve((lambda Ax, nw_: (lambda e: e.tensor_tensor(
                out=V(Ax, 0, [[4, 8], [1, 4]]), in0=V(Ax, 0, [[4, 8], [1, 4]]),
                in1=V(nw_, 0, [[1, 8], [0, 4]]), op=ALU.mult)))(Ax, nw_),
                reads=[B_mod, B_const], writes=[B_mod])

        mod_part(0, 4)
        mod_A(A1, nmw, 8)
        conv(win_d, win_s, 10, "win")
        Q.dve(lambda e: e.tensor_tensor(out=lbp[:], in0=V(lbl, 0, [[1, 4]]), in1=V(lbl, 4, [[1, 4]]), op=ALU.subtract),
              reads=[B_const], writes=[B_mod])
        Q.act(lambda e: e.activation(out=lbp[:], in_=lbp[:], func=AF.Tanh, scale=0.5), reads=[B_mod], writes=[B_mod])
        Q.dve(lambda e: e.tensor_scalar(out=omlh[:], in0=lbp[:], scalar1=-0.25, scalar2=0.25, op0=ALU.mult, op1=ALU.add),
              reads=[B_mod], writes=[B_mod])
        Q.dve(lambda e: e.tensor_scalar(out=nomlh[:], in0=lbp[:], scalar1=0.25, scalar2=-0.25, op0=ALU.mult, op1=ALU.add),
              reads=[B_mod], writes=[B_mod])
        Q.dve(lambda e: e.tensor_scalar(out=lbp[:], in0=lbp[:], scalar1=0.25, scalar2=0.75, op0=ALU.mult, op1=ALU.add),
              reads=[B_mod], writes=[B_mod])
        Q.dve(lambda e: e.tensor_scalar(out=gnwc[:], in0=gnw[:], scalar1=CQ, scalar2=None, op0=ALU.mult),
              reads=[B_const], writes=[B_mod])


        MIX_SEQ = [("win", 2), ("win", 0), ("win", 3), ("win", 1), ("win", 6), ("win", 7), ("win", 4), ("win", 5),
                   ("win", 8), ("win", 9), ("wout", 0), ("wout", 1), ("wout", 2), ("wout", 3)]
        MLP_SEQ = [("wup", i) for i in range(16)] + [("wdn", i) for i in range(16)]
        W_SRC = {"win": win_s, "wout": wout_s, "wup": wup_s, "wdn": wdn_s}

        class WRing:
            def __init__(self, slots, seq, total):
                self.free = list(slots)
                self.seq = seq
                self.total = total
                self.next_load = 0
                self.next_acq = 0
                self.loaded = {}

            def _issue(self):
                if self.next_load >= self.total or not self.free:
                    return
                i = self.free.pop(0)
                kind, pc = self.seq[self.next_load % len(self.seq)]
                src = W_SRC[kind]
                Q.dma("sp", (lambda i, pc, src: (lambda e: e.dma_start(out=wsl[i][:], in_=src.ap()[pc])))(i, pc, src),
                      reads=[B_scr[kind][pc]], writes=[B_w[i]], key="w%d" % i)
                self.loaded[self.next_load] = i
                self.next_load += 1

            def prime(self):
                while self.free and self.next_load < self.total:
                    self._issue()

            def acquire(self, kind, pc):
                n = self.next_acq
                self.next_acq += 1
                assert self.seq[n % len(self.seq)] == (kind, pc), (n, kind, pc)
                if n not in self.loaded:
                    self._issue()
                return self.loaded.pop(n)

            def release(self, i):
                self.free.append(i)
                self._issue()

        ring_mix = WRing([0, 1, 2], MIX_SEQ, NT * len(MIX_SEQ))
        ring_mlp = WRing([3, 4, 5], MLP_SEQ, NT * len(MLP_SEQ))

        def WS(i, k, off, n):
            return V(wsl[i], k * 256 + off, [[1, n]])

        def norm_stats(slots, which):
            if which == 2:
                r = 2
            else:
                r = state["ssr"] % 2
                state["ssr"] += 1
            for j, s in enumerate(slots):
                if which == 2:
                    dump, Bd = r32[0][:].bitcast(BF16), B_r32[0]
                else:
                    dump, Bd = V(xn, j * D, [[1, D]]), B_xn[j]
                Q.act((lambda s, j, dump: (lambda e: e.activation(out=dump, in_=V(xh, s * D, [[1, D]]), func=AF.Square,
                                                                  accum_out=V(ss[r], j, [[1, 1]]))))(s, j, dump),
                      reads=[B_xh[s]], writes=[B_ss[r], Bd])
            Q.act(lambda e: e.activation(out=rst[r][:], in_=ss[r][:], func=AF.Ln, bias=EPSC(), scale=1.0 / D),
                  reads=[B_ss[r], B_eps], writes=[B_rst[r]])
            Q.act(lambda e: e.activation(out=rst[r][:], in_=rst[r][:], func=AF.Exp, scale=-0.5),
                  reads=[B_rst[r]], writes=[B_rst[r]])
            return r

        epsc = sb("epsc", [128, 1], F32)
        Q.pool(lambda e: e.memset(epsc[:], EPS), writes=[B_eps])
        EPSC = lambda: epsc[:]

        def norm_to_T(t, slots, Ax, shbase, dstT, B_dst):
            b = t // 4
            r = norm_stats(slots, 0)
            for j, s in enumerate(slots):
                Q.dve((lambda s, j: (lambda e: e.tensor_scalar(
                    out=V(xn, j * D, [[1, D]]), in0=V(xh, s * D, [[1, D]]), scalar1=V(rst[r], j, [[1, 1]]),
                    scalar2=None, op0=ALU.mult)))(s, j),
                    reads=[B_xh[s], B_rst[r]], writes=[B_xn[j]])
            Q.pad(3)
            for kp in range(4):
                bk = mix_bank()
                for j in range(4):
                    for kk_ in range(2):
                        k = 2 * kp + kk_
                        Q.pe((lambda j, kk_, k, bk: (lambda e: e.transpose(
                            pbf(bk, (kk_ * 4 + j) * 128, [[1, 128]]), V(xn, j * D + k * 128, [[1, 128]]), IDB())))(j, kk_, k, bk),
                            reads=[B_xn[j], B_const], writes=[B_pb[bk]])
                for kk_ in range(2):
                    k = 2 * kp + kk_
                    if kk_ == 0:
                        Q.act((lambda k, kk_, bk: (lambda e: e.activation(
                            out=V(dstT, k * TT, [[1, TT]]), in_=pbf(bk, kk_ * 512, [[1, 512]]), func=AF.Identity,
                            bias=V(modT, (shbase + k) * 4 + b, [[1, 1]]), scale=V(Ax, k * 4 + b, [[1, 1]]))))(k, kk_, bk),
                            reads=[B_pb[bk], B_mod], writes=[B_dst[k]])
                    else:
                        Q.dve((lambda k, kk_, bk: (lambda e: e.tensor_scalar(
                            out=V(dstT, k * TT, [[1, TT]]), in0=pbf(bk, kk_ * 512, [[1, 512]]),
                            scalar1=V(Ax, k * 4 + b, [[1, 1]]), scalar2=V(modT, (shbase + k) * 4 + b, [[1, 1]]),
                            op0=ALU.mult, op1=ALU.add)))(k, kk_, bk),
                            reads=[B_pb[bk], B_mod], writes=[B_dst[k]])
                Q.cut()

        def fm_chunk(wi, off, srcT, B_src, bk):
            for k in range(8):
                Q.pe((lambda k: (lambda e: e.matmul(pb[bk][:], lhsT=WS(wi, k, off, 128), rhs=V(srcT, k * TT, [[1, TT]]),
                                                    start=(k == 0), stop=(k == 7))))(k),
                     reads=[B_w[wi], B_src[k]], writes=[B_pb[bk]])

        def stage_N1(t):
            P.label = "N1%d" % t
            slots = [(4 * t + j) % NXS for j in range(4)]
            norm_to_T(t, slots, A1, 0, ucT, B_uc)

        def stage_A(t):
            P.label = "A%d" % t
            seq_first = (t % 4 == 0)
            wq = [None, None]
            wqq = [None, None]
            for h in range(4):
                if h % 2 == 0:
                    wq[h // 2] = ring_mix.acquire("win", 2 + h // 2)
                    wqq[h // 2] = ring_mix.acquire("win", h // 2)
                bk = mix_bank()
                fm_chunk(wq[h // 2], (h % 2) * 128, ucT, B_uc, bk)
                if h % 2 == 1:
                    ring_mix.release(wq[h // 2])
                Q.act((lambda bk: (lambda e: e.activation(out=th[:], in_=pb[bk][:], func=AF.Tanh, scale=0.5)))(bk),
                      reads=[B_pb[bk]], writes=[B_th])
                Q.dve((lambda h: (lambda e: e.tensor_scalar(out=kk32[:], in0=th[:], scalar1=V(nomlh, h, [[1, 1]]),
                                                            scalar2=V(omlh, h, [[1, 1]]), op0=ALU.mult, op1=ALU.add)))(h),
                      reads=[B_th, B_mod], writes=[B_kk])
                Q.dve((lambda h: (lambda e: e.tensor_scalar(out=fg[:], in0=th[:], scalar1=V(omlh, h, [[1, 1]]),
                                                            scalar2=V(lbp, h, [[1, 1]]), op0=ALU.mult, op1=ALU.add)))(h),
                      reads=[B_th, B_mod], writes=[B_fg])
                Q.cut()
                bk2 = mix_bank()
                fm_chunk(wqq[h // 2], (h % 2) * 128, ucT, B_uc, bk2)
                if h % 2 == 1:
                    ring_mix.release(wqq[h // 2])
                Q.act((lambda bk2: (lambda e: e.activation(out=qs32[:], in_=pb[bk2][:], func=AF.Silu)))(bk2),
                      reads=[B_pb[bk2]], writes=[B_qs])
                Q.cut()
                Q.act(lambda e: e.activation(out=fg[:], in_=fg[:], func=AF.Ln), reads=[B_fg], writes=[B_fg])
                Q.dve(lambda e: e.tensor_tensor_scan(out=bb[:], data0=scanm[:], data1=fg[:], initial=0.0,
                                                     op0=ALU.mult, op1=ALU.add),
                      reads=[B_fg, B_const], writes=[B_bb])
                Q.dve(lambda e: e.tensor_tensor(out=V(dd, 0, [[32, 16], [1, 32]]), in0=V(bb, 0, [[32, 16], [1, 32]]),
                                                in1=V(bb, 15, [[32, 16], [0, 32]]), op=ALU.subtract),
                      reads=[B_bb], writes=[B_dd])
                Q.dve(lambda e: e.tensor_tensor(out=V(fg, 0, [[32, 16], [1, 32]]), in0=V(bb, 31, [[32, 16], [0, 32]]),
                                                 in1=V(bb, 0, [[32, 16], [1, 32]]), op=ALU.subtract),
                       reads=[B_bb, B_fg], writes=[B_fg])
                Q.cut()
                Q.act(lambda e: e.activation(out=Et[0][:], in_=dd[:], func=AF.Exp), reads=[B_dd], writes=[B_E[0]])
                Q.dve((lambda h: (lambda e: e.tensor_tensor(out=V(qtil, h * TT, [[1, TT]]), in0=qs32[:], in1=Et[0][:],
                                                            op=ALU.mult)))(h),
                      reads=[B_qs, B_E[0]], writes=[B_qtil[h]])
                Q.act(lambda e: e.activation(out=Et[1][:], in_=dd[:], func=AF.Exp, scale=-1.0), reads=[B_dd], writes=[B_E[1]])
                Q.dve((lambda h: (lambda e: e.tensor_tensor(out=V(ktil, h * TT, [[1, TT]]), in0=kk32[:], in1=Et[1][:],
                                                             op=ALU.mult)))(h),
                       reads=[B_kk, B_E[1]], writes=[B_ktil[h]])
                Q.cut()
                Q.act(lambda e: e.activation(out=Et[0][:], in_=bb[:], func=AF.Exp), reads=[B_bb], writes=[B_E[0]])
                Q.dve((lambda h: (lambda e: e.tensor_tensor(out=V(qin, h * TT, [[1, TT]]), in0=qs32[:], in1=Et[0][:],
                                                            op=ALU.mult)))(h),
                      reads=[B_qs, B_E[0]], writes=[B_qin[h]])
                Q.dve((lambda h: (lambda e: e.tensor_copy(out=V(decs, h * 16, [[1, 16]]), in_=V(Et[0], 31, [[32, 16]]))))(h),
                      reads=[B_E[0]], writes=[B_dec[h]])
                Q.act(lambda e: e.activation(out=Et[1][:], in_=fg[:], func=AF.Exp), reads=[B_fg], writes=[B_E[1]])
                Q.dve((lambda h: (lambda e: e.tensor_tensor(out=V(koutT, h * TT, [[1, TT]]), in0=kk32[:], in1=Et[1][:],
                                                             op=ALU.mult)))(h),
                       reads=[B_kk, B_E[1]], writes=[B_koutT[h]])
                Q.cut()
            wg = [None, None]
            for h in range(4):
                if h % 2 == 0:
                    wg[h // 2] = ring_mix.acquire("win", 6 + h // 2)
                bk = mix_bank()
                fm_chunk(wg[h // 2], (h % 2) * 128, ucT, B_uc, bk)
                if h % 2 == 1:
                    ring_mix.release(wg[h // 2])
                Q.act((lambda h, bk: (lambda e: e.activation(out=V(sg, h * TT, [[1, TT]]), in_=pb[bk][:], func=AF.Silu)))(h, bk),
                      reads=[B_pb[bk]], writes=[B_sg[h]])
                Q.cut()
            for (pcs, dst, Bd, nm) in ((4, vtok, B_v, "v"), (8, ptok, B_p, "p")):
                wv = [ring_mix.acquire("win", pcs), ring_mix.acquire("win", pcs + 1)]
                for j in range(4):
                    bk = mix_bank()
                    for half in range(2):
                        for k in range(8):
                            Q.pe((lambda j, half, k, bk, wvh: (lambda e: e.matmul(
                                V(pb[bk], half * 256, [[1, 256]]), lhsT=V(ucT, k * TT + j * 128, [[1, 128]]),
                                rhs=WS(wvh, k, 0, 256), start=(k == 0), stop=(k == 7))))(j, half, k, bk, wv[half]),
                                reads=[B_w[wv[half]], B_uc[k]], writes=[B_pb[bk]])
                    if nm == "v":
                        dslot, Bds = j, Bd[j]
                    else:
                        dslot = (4 * t + j) % 5
                        Bds = Bd[dslot]
                    if j % 2 == 0:
                        Q.act((lambda dslot, bk, dst: (lambda e: e.activation(
                            out=V(dst, dslot * TT, [[1, TT]]), in_=pb[bk][:], func=AF.Copy)))(dslot, bk, dst),
                            reads=[B_pb[bk]], writes=[Bds])
                    else:
                        Q.dve((lambda dslot, bk, dst: (lambda e: e.tensor_copy(
                            out=V(dst, dslot * TT, [[1, TT]]), in_=pb[bk][:])))(dslot, bk, dst),
                            reads=[B_pb[bk]], writes=[Bds])
                    if j == 3:
                        ring_mix.release(wv[0])
                        ring_mix.release(wv[1])
                    Q.cut()

        def stage_B(t):
            P.label = "B%d" % t
            seq_first = (t % 4 == 0)
            for j in range(4):
                bsc = 4 + (j % 2)
                for h in range(4):
                    Q.pe((lambda h, j, bsc: (lambda e: e.matmul(
                        V(pb[bsc], h * 128, [[1, 128]]), lhsT=V(ktil, h * TT + j * 128, [[1, 128]]),
                        rhs=V(qtil, h * TT + j * 128, [[1, 128]]), start=True, stop=True)))(h, j, bsc),
                        reads=[B_ktil[h], B_qtil[h]], writes=[B_pb[bsc]])
                Q.dve((lambda bsc, j: (lambda e: e.tensor_tensor(
                    out=V(scTm[j], 0, [[128, 4], [1, 128]]), in0=V(pb[bsc], 0, [[128, 4], [1, 128]]),
                    in1=V(cbt, 256, [[0, 4], [1, 128]]), op=ALU.mult)))(bsc, j),
                    reads=[B_pb[bsc], B_const], writes=[B_scTm[j]])
                Q.cut()
                bkt = 6 + (j % 2)
                for h in range(4):
                    Q.pe((lambda h, j, bkt: (lambda e: e.transpose(
                        pbf(bkt, h * 128, [[1, 128]]), V(koutT, h * TT + j * 128, [[1, 128]]), IDB())))(h, j, bkt),
                        reads=[B_koutT[h], B_const], writes=[B_pb[bkt]])
                Q.act((lambda bkt, j: (lambda e: e.activation(out=V(koutk[j], 0, [[1, 512]]), in_=pbf(bkt, 0, [[1, 512]]),
                                                              func=AF.Copy)))(bkt, j),
                      reads=[B_pb[bkt]], writes=[B_koutk[j]])
                Q.cut()
            for j in range(4):
                first128 = seq_first and j == 0
                bp = 4 + (j % 4)
                ps_ = (4 * t + j) % 5
                pp_ = (4 * t + j - 1) % 5
                for g in range(4):
                    Q.pe((lambda g, bp, ps_, first128: (lambda e: e.matmul(
                        V(pb[bp], g * 128, [[1, 128]]), lhsT=V(ptok, ps_ * TT + g * 128, [[1, 128]]),
                        rhs=BAND((4 + g) if first128 else g), start=True, stop=first128, skip_group_check=True)))(g, bp, ps_, first128),
                        reads=[B_p[ps_], B_const], writes=[B_pb[bp]])
                    if not first128:
                        Q.pe((lambda g, bp, pp_: (lambda e: e.matmul(
                            V(pb[bp], g * 128, [[1, 16]]), lhsT=V(ptok, pp_ * TT + g * 128, [[1, 128]]),
                            rhs=BAND(8 + g, 16), start=False, stop=True, skip_group_check=True)))(g, bp, pp_),
                            reads=[B_p[pp_], B_const], writes=[B_pb[bp]])
                Q.dve((lambda j, bp: (lambda e: e.tensor_copy(
                    out=V(pooledT, j * 128, [[TT, 4], [1, 128]]), in_=V(pb[bp], 0, [[128, 4], [1, 128]]))))(j, bp),
                    reads=[B_pb[bp]], writes=[B_pooled[j]])
                Q.cut()
            for g in range(4):
                bx = 4 + (g % 4)
                Q.pe((lambda g, bx: (lambda e: e.matmul(pb[bx][:], lhsT=V(wpool, g * 128, [[1, 128]]),
                                                        rhs=V(pooledT, g * TT, [[1, TT]]), start=True, stop=True)))(g, bx),
                     reads=B_pooled + [B_const], writes=[B_pb[bx]])
                Q.act((lambda g, bx: (lambda e: e.activation(out=V(ucT, (4 + g) * TT, [[1, TT]]), in_=pb[bx][:], func=AF.Copy,
                                                             scale=V(psc, g, [[1, 1]]))))(g, bx),
                      reads=[B_pb[bx], B_const], writes=[B_uc[4 + g]])
                Q.cut()

            svs_all = {}

            def chain(j):
                first128 = seq_first and j == 0
                if first128:
                    Q.pool(lambda e: e.memset(S32[:], 0.0), writes=[B_S32])
                    sv0 = state["sbf"] % NSB
                    Q.pool((lambda sv0: (lambda e: e.memset(Sbf[sv0][:], 0.0)))(sv0), writes=[B_Sbf[sv0]])
                svs = [state["sbf"] % NSB]
                for jj in range(4):
                    c = j * 4 + jj
                    bds = 4 + (state["dsb"] % 2)
                    state["dsb"] += 1
                    for h in range(4):
                        Q.pe((lambda h, jj, j, bds: (lambda e: e.matmul(
                            V(pb[bds], h * 128, [[1, 128]]), lhsT=V(koutk[j], h * 128, [[1, 128]], p0=32 * jj, n=32),
                            rhs=V(vtok, j * TT + h * 128, [[1, 128]], p0=32 * jj, n=32), start=True, stop=True,
                            tile_position=(32 * jj, 0))))(h, jj, j, bds),
                            reads=[B_koutk[j], B_v[j]], writes=[B_pb[bds]])
                    for h in range(4):
                        Q.dve((lambda h, c, bds: (lambda e: e.scalar_tensor_tensor(
                            out=V(S32, h * 128, [[1, 128]]), in0=V(S32, h * 128, [[1, 128]]),
                            scalar=V(decs, h * 16 + c, [[1, 1]]), in1=V(pb[bds], h * 128, [[1, 128]]),
                            op0=ALU.mult, op1=ALU.add)))(h, c, bds),
                            reads=[B_pb[bds], B_dec[h]], writes=[B_S32])
                    state["sbf"] += 1
                    sv = state["sbf"] % NSB
                    svs.append(sv)
                    Q.act((lambda sv: (lambda e: e.activation(out=V(Sbf[sv], 0, [[1, 512]]), in_=V(S32, 0, [[1, 512]]),
                                                              func=AF.Copy)))(sv),
                          reads=[B_S32], writes=[B_Sbf[sv]])
                    Q.pad(1)
                svs_all[j] = svs

            def oT(j):
                bo = 6 + (j % 2)
                svs = svs_all[j]
                for h in range(4):
                    Q.pe((lambda h, j, bo: (lambda e: e.matmul(
                        V(pb[bo], h * 128, [[1, 128]]), lhsT=V(vtok, j * TT + h * 128, [[1, 128]]),
                        rhs=V(scTm[j], h * 128, [[1, 128]]), start=True, stop=False, skip_group_check=True)))(h, j, bo),
                        reads=[B_v[j], B_scTm[j]], writes=[B_pb[bo]])
                    for jj in range(4):
                        sv = svs[jj]
                        Q.pe((lambda h, j, jj, bo, sv: (lambda e: e.matmul(
                            V(pb[bo], h * 128 + jj * 32, [[1, 32]]), lhsT=V(Sbf[sv], h * 128, [[1, 128]]),
                            rhs=V(qin, h * TT + j * 128 + jj * 32, [[1, 32]]), start=False, stop=(jj == 3),
                            skip_group_check=True)))(h, j, jj, bo, sv),
                            reads=[B_Sbf[sv], B_qin[h]], writes=[B_pb[bo]])
                Q.act((lambda bo: (lambda e: e.activation(out=osq[0][:], in_=pb[bo][:], func=AF.Square, scale=CQ)))(bo),
                      reads=[B_pb[bo]], writes=[B_osq[0]])
                Q.cut()

            def onorm(j):
                bo = 6 + (j % 2)
                bm = 4 + (state["dsb"] % 2)
                state["dsb"] += 1
                Q.pe((lambda bm: (lambda e: e.matmul(pb[bm][:], lhsT=ONES(), rhs=osq[0][:], start=True, stop=True)))(bm),
                     reads=[B_osq[0], B_const], writes=[B_pb[bm]])
                Q.act((lambda bm: (lambda e: e.activation(out=rsb[0][:], in_=pb[bm][:], func=AF.Ln, bias=EPSC())))(bm),
                      reads=[B_pb[bm], B_eps], writes=[B_rsb[0]])
                Q.act(lambda e: e.activation(out=rsb[0][:], in_=rsb[0][:], func=AF.Exp, scale=-0.5),
                      reads=[B_rsb[0]], writes=[B_rsb[0]])
                Q.dve((lambda bo: (lambda e: e.tensor_tensor(out=rsb[0][:], in0=pb[bo][:], in1=rsb[0][:], op=ALU.mult)))(bo),
                      reads=[B_pb[bo], B_rsb[0]], writes=[B_rsb[0]])
                Q.dve((lambda j: (lambda e: e.scalar_tensor_tensor(
                    out=V(ucT, j * 128, [[TT, 4], [1, 128]]), in0=V(rsb[0], 0, [[128, 4], [1, 128]]), scalar=gnwc[:],
                    in1=V(sg, j * 128, [[TT, 4], [1, 128]]), op0=ALU.mult, op1=ALU.mult)))(j),
                    reads=[B_rsb[0], B_mod] + B_sg, writes=B_uc[0:4])
                Q.cut()

            chain(0)
            chain(1)
            oT(0)
            chain(2)
            onorm(0)
            oT(1)
            chain(3)
            onorm(1)
            oT(2)
            Q.pad(2)
            onorm(2)
            oT(3)
            Q.pad(2)
            onorm(3)

        def back_transpose_add(src, B_src, m, slots, bank_fn):
            bt = bank_fn()
            for j in range(4):
                Q.pe((lambda j, bt: (lambda e: e.transpose(V(pb[bt], j * 128, [[1, 128]]), V(src, j * 128, [[1, 128]]), idf[:])))(j, bt),
                     reads=[B_src, B_const], writes=[B_pb[bt]])
            s0 = slots[0]
            Q.dve((lambda bt, s0, m: (lambda e: e.tensor_tensor(
                out=V(xh, s0 * D + m * 128, [[D, 4], [1, 128]]), in0=V(xh, s0 * D + m * 128, [[D, 4], [1, 128]]),
                in1=V(pb[bt], 0, [[128, 4], [1, 128]]), op=ALU.add)))(bt, s0, m),
                reads=[B_pb[bt]], writes=[B_xh[s] for s in slots])

        def stage_C(t):
            P.label = "C%d" % t
            b = t // 4
            slots = [(4 * t + j) % NXS for j in range(4)]
            wi = None
            for m in range(8):
                if m % 2 == 0:
                    wi = ring_mix.acquire("wout", m // 2)
                bk = mix_bank()
                fm_chunk(wi, (m % 2) * 128, ucT, B_uc, bk)
                if m % 2 == 1:
                    ring_mix.release(wi)
                r = m % 2
                Q.act((lambda bk, r, m: (lambda e: e.activation(out=mg[r][:], in_=pb[bk][:], func=AF.Copy,
                                                                scale=V(modT, (16 + m) * 4 + b, [[1, 1]]))))(bk, r, m),
                      reads=[B_pb[bk], B_mod], writes=[B_mg[r]])
                if m >= 1:
                    back_transpose_add(mg[(m - 1) % 2], B_mg[(m - 1) % 2], m - 1, slots, mix_bank)
                Q.cut()
            Q.pad(1)
            back_transpose_add(mg[1], B_mg[1], 7, slots, mix_bank)
            Q.cut()

        def stage_N2(t):
            P.label = "N2%d" % t
            slots = [(4 * t + j) % NXS for j in range(4)]
            norm_to_T(t, slots, A2, 24, u2T, B_u2)

        def stage_UP(t):
            P.label = "UP%d" % t
            wi = None
            for m in range(32):
                if m % 2 == 0:
                    wi = ring_mlp.acquire("wup", m // 2)
                bk = mlp_bank()
                fm_chunk(wi, (m % 2) * 128, u2T, B_u2, bk)
                if m % 2 == 1:
                    ring_mlp.release(wi)
                r = m % 2
                Q.act((lambda bk, r: (lambda e: e.activation(out=r32[r][:], in_=pb[bk][:], func=AF.Relu)))(bk, r),
                      reads=[B_pb[bk]], writes=[B_r32[r]])
                Q.dve((lambda bk, r, m: (lambda e: e.scalar_tensor_tensor(
                    out=V(hidT, m * TT, [[1, TT]]), in0=pb[bk][:], scalar=0.0, in1=r32[r][:],
                    op0=ALU.max, op1=ALU.mult)))(bk, r, m),
                    reads=[B_pb[bk], B_r32[r]], writes=[B_hid[m]])
                Q.cut()

        def stage_DOWN(t):
            P.label = "DOWN%d" % t
            b = t // 4
            slots = [(4 * t + j) % NXS for j in range(4)]
            pend = []
            for mp in range(4):
                bks = [mlp_bank(), mlp_bank()]
                for kq in range(4):
                    wi = ring_mlp.acquire("wdn", mp * 4 + kq)
                    for k in range(8):
                        for half in range(2):
                            Q.pe((lambda k, half, kq, wi, bkh: (lambda e: e.matmul(
                                pb[bkh][:], lhsT=WS(wi, k, half * 128, 128), rhs=V(hidT, (kq * 8 + k) * TT, [[1, TT]]),
                                start=(kq == 0 and k == 0), stop=(kq == 3 and k == 7))))(k, half, kq, wi, bks[half]),
                                reads=[B_w[wi], B_hid[kq * 8 + k]], writes=[B_pb[bks[half]]])
                    ring_mlp.release(wi)
                    Q.cut()
                if pend:
                    mq = pend.pop(0)
                    for half in range(2):
                        back_transpose_add(mg[2 + half], B_mg[2 + half], 2 * mq + half, slots, mlp_bank)
                for half in range(2):
                    m = 2 * mp + half
                    Q.act((lambda half, m, bkh: (lambda e: e.activation(out=mg[2 + half][:], in_=pb[bkh][:], func=AF.Copy,
                                                                   scale=V(modT, (40 + m) * 4 + b, [[1, 1]]))))(half, m, bks[half]),
                          reads=[B_pb[bks[half]], B_mod], writes=[B_mg[2 + half]])
                pend.append(mp)
                Q.cut()
            Q.pad(1)
            mq = pend.pop(0)
            for half in range(2):
                back_transpose_add(mg[2 + half], B_mg[2 + half], 2 * mq + half, slots, mlp_bank)
            Q.cut()

        def stage_OUT(t):
            P.label = "OUT%d" % t
            slots = [(4 * t + j) % NXS for j in range(4)]
            r = norm_stats(slots, 2)
            for j, s in enumerate(slots):
                Q.dve((lambda s, j: (lambda e: e.scalar_tensor_tensor(
                    out=V(xh, s * D, [[1, D]]), in0=V(xh, s * D, [[1, D]]), scalar=V(rst[r], j, [[1, 1]]),
                    in1=wfb[:], op0=ALU.mult, op1=ALU.mult)))(s, j),
                    reads=[B_xh[s], B_rst[r], B_const], writes=[B_xh[s]])
                r0 = t * TT + j * 128
                Q.dma("pool", (lambda s, r0: (lambda e: e.dma_start(out=out_d.ap()[r0:r0 + 128, :], in_=V(xh, s * D, [[1, D]]))))(s, r0),
                      reads=[B_xh[s]], key="o%d" % s)
                Q.cut()

        for j in range(4):
            if NT > 1:
                x_load(1, j)
        conv(wout_d, wout_s, 4, "wout")
        ring_mix.prime()
        stage_N1(0)
        stage_A(0)
        mod_part(4, 6)
        stage_B(0)
        stage_C(0)
        mod_part(6, 12)
        mod_A(A2, nlw, 32)
        conv(wup_d, wup_s, 16, "wup")
        conv(wdn_d, wdn_s, 16, "wdn")
        ring_mlp.prime()
        stage_N2(0)
        for t in range(NT):
            Q.begin()
            stage_UP(t)
            stage_DOWN(t)
            stage_OUT(t)
            s_mlp = Q.end()
            s_mix = []
            if t + 1 < NT:
                Q.begin()
                stage_N1(t + 1)
                stage_A(t + 1)
                stage_B(t + 1)
                stage_C(t + 1)
                stage_N2(t + 1)
                s_mix = Q.end()
            run_steps(merge(s_mlp, s_mix, lead=LEAD, tail=TAIL, first_b=0))
            if t + 2 < NT:
                for j in range(4):
                    x_load(t + 2, j)

        if debug:
            pass

        okeys = sorted(k for k in P.dma_counts if isinstance(k, str) and k.startswith("o"))
        P.emit(nc, final_dma_keys=okeys)
        build_nc.labels = {e: [o.label for o in P.ops[e]] for e in ENGS}
        build_nc.stats = {e: (len(P.ops[e]), sum(1 for o in P.ops[e] if o.signal)) for e in ENGS}
    return nc


def make_in_maps(inputs, ncores=NCORES):
    f = lambda a: np.ascontiguousarray(np.asarray(a, dtype=np.float32))
    x = f(inputs["x"])
    c = f(inputs["c"])
    cb, scan, ident = _consts()
    shared = {
        "w_ada": f(inputs["w_ada"][0]),
        "b_ada": f(inputs["b_ada"][0].reshape(48, 128).T),
        "nmw": f(inputs["norm_mix_w"][0].reshape(8, 128).T),
        "nlw": f(inputs["norm_mlp_w"][0].reshape(8, 128).T),
        "wf": f(np.broadcast_to(np.asarray(inputs["norm_final_w"], np.float32)[None, :], (128, D))),
        "w_in": f(inputs["w_in"][0]),
        "w_out": f(inputs["w_out"][0]),
        "w_up": f(inputs["w_up"][0]),
        "w_down": f(inputs["w_down"][0]),
        "lbl": f(np.asarray(inputs["lb_logits"], np.float32).reshape(2, 4, 128).transpose(2, 0, 1)),
        "gnw": f(np.asarray(inputs["g_norm_w"], np.float32)[0].reshape(128, 1)),
        "wpool": f(np.asarray(inputs["w_pool"], np.float32)[0].transpose(1, 0, 2)),
        "psc": f(np.asarray(inputs["pool_scale"], np.float32)[0].reshape(4, 128).T),
        "cb": cb, "scanm": scan, "idf": ident,
    }
    maps = []
    for i in range(ncores):
        m = dict(shared)
        m["x"] = np.ascontiguousarray(x[4 * i:4 * i + 4].reshape(TOK, D))
        m["cT"] = f(c[4 * i:4 * i + 4].T.reshape(8, 128, 4).transpose(1, 0, 2))
        maps.append(m)
    return maps


_NC_CACHE = {}


def kernel(**inputs):
    if "nc" not in _NC_CACHE:
        _NC_CACHE["nc"] = build_nc()
    nc = _NC_CACHE["nc"]
    maps = make_in_maps(inputs)
    res = run_bass_kernel_spmd(nc, maps, core_ids=list(range(NCORES)))
    outs = [np.asarray(r["out"], dtype=np.float32).reshape(4, SEQ, D) for r in res.results]
    return np.concatenate(outs, axis=0)
```
